# Optimizing a Trainium2 kernel written in Bass

```python
import math
import numpy as np
import jax
import jax.numpy as jnp
from jax import lax

D_MODEL = 1024
BATCH = 8
SEQ = 4096
DEPTH = 4

EPS = 1e-6
ROPE_THETA = 10000.0
MASK_VALUE = -1e30

DN_HEADS = 4
DN_DK = 128
DN_DV = 128
DN_CONV = 4
DN_CHUNK = 64
SC_CHANNELS = 512
SC_CONV = 3
HG_HEADS = 4
HG_DK = 128
HG_DV = 128
HG_CHUNK = 64
SWA_PATTERNS = ((128, 1), (512, 4), (2048, 16))
SWA_HEADS = 4
SWA_DH = 128
FFN_HIDDEN = -(-8 * D_MODEL // (3 * 256)) * 256

DN_W = DN_HEADS * DN_DK
DN_VW = DN_HEADS * DN_DV
AB_SPLITS = (2 * DN_W + DN_VW, DN_VW, DN_HEADS, DN_HEADS, SC_CHANNELS, SC_CHANNELS, SC_CHANNELS)
AB_IN = sum(AB_SPLITS)
AB_OUT = DN_VW + SC_CHANNELS
HG_W = HG_HEADS * HG_DK
HG_VW = HG_HEADS * HG_DV
SWA_W = len(SWA_PATTERNS) * SWA_HEADS * SWA_DH
CD_SPLITS = (HG_W, HG_W, HG_VW, HG_VW, SWA_W, SWA_W, SWA_W)
CD_IN = sum(CD_SPLITS)
CD_OUT = HG_VW + SWA_HEADS * SWA_DH
N_EVEN = (DEPTH + 1) // 2
N_ODD = DEPTH // 2

kernel_name = 'hybrid_deltanet_shortconv_hgrn2_dilated_swa'


def rms_norm(x, g):
    xf = x.astype(jnp.float32)
    y = xf * lax.rsqrt(jnp.mean(xf * xf, axis=-1, keepdims=True) + EPS)
    return (y * g.astype(jnp.float32)).astype(x.dtype)


def l2_normalize(x):
    xf = x.astype(jnp.float32)
    return xf * lax.rsqrt(jnp.sum(xf * xf, axis=-1, keepdims=True) + EPS)


def split_cols(a, sizes):
    return jnp.split(a, [int(s) for s in np.cumsum(sizes)[:-1]], axis=-1)


def causal_depthwise_conv(x, w):
    width = w.shape[0]
    t = x.shape[1]
    xp = jnp.pad(x, ((0, 0), (width - 1, 0), (0, 0)))
    return sum(xp[:, j:j + t] * w[j] for j in range(width))


def masked_exp(mask, logits):
    return jnp.where(mask, jnp.exp(jnp.where(mask, logits, 0.0)), 0.0)


def rope(x, pos):
    dh = x.shape[-1]
    half = dh // 2
    inv_freq = ROPE_THETA ** (-jnp.arange(half, dtype=jnp.float32) * 2.0 / dh)
    ang = pos.astype(jnp.float32)[:, None] * inv_freq[None, :]
    shape = (1, x.shape[1]) + (1,) * (x.ndim - 3) + (half,)
    cos = jnp.cos(ang).reshape(shape)
    sin = jnp.sin(ang).reshape(shape)
    xf = x.astype(jnp.float32)
    x1, x2 = xf[..., :half], xf[..., half:]
    return jnp.concatenate([x1 * cos - x2 * sin, x1 * sin + x2 * cos], axis=-1).astype(x.dtype)


def to_chunks(a, c):
    b, t = a.shape[:2]
    a = a.reshape((b, t // c, c) + a.shape[2:])
    return jnp.moveaxis(jnp.moveaxis(a, 1, 0), 2, 3)


def from_chunks(a):
    a = jnp.moveaxis(jnp.moveaxis(a, 3, 2), 0, 1)
    n_b, n, c, h, d = a.shape
    return a.reshape(n_b, n * c, h, d)


def gated_delta_rule(q, k, v, g, beta):
    f32 = jnp.float32
    b, t, h, dk = q.shape
    dv = v.shape[-1]
    c = DN_CHUNK
    q, k, v = (to_chunks(a.astype(f32), c) for a in (q, k, v))
    gc = jnp.cumsum(to_chunks(g.astype(f32), c), axis=-1)
    bt = to_chunks(beta.astype(f32), c)
    causal = jnp.tril(jnp.ones((c, c), bool))
    decay = masked_exp(causal, gc[..., :, None] - gc[..., None, :])
    kk = jnp.einsum('nbhcd,nbhsd->nbhcs', k, k)
    a_strict = jnp.where(jnp.eye(c, dtype=bool), 0.0, bt[..., None] * kk * decay)
    rhs = jnp.concatenate([v * bt[..., None], k * (bt * jnp.exp(gc))[..., None]], axis=-1)
    sol = lax.linalg.triangular_solve(a_strict + jnp.eye(c, dtype=f32), rhs,
                                      left_side=True, lower=True, unit_diagonal=True)
    u_base, w = sol[..., :dv], sol[..., dv:]
    qk = jnp.einsum('nbhcd,nbhsd->nbhcs', q, k) * decay
    q_dec = q * jnp.exp(gc)[..., None]
    k_dec = k * jnp.exp(gc[..., -1:] - gc)[..., None]
    g_tot = jnp.exp(gc[..., -1])

    def step(state, inp):
        u_b, w_c, qk_c, qd_c, kd_c, gt_c = inp
        u = u_b - jnp.einsum('bhcd,bhde->bhce', w_c, state)
        o = jnp.einsum('bhcd,bhde->bhce', qd_c, state) + jnp.einsum('bhcs,bhse->bhce', qk_c, u)
        state = state * gt_c[..., None, None] + jnp.einsum('bhcd,bhce->bhde', kd_c, u)
        return state, o

    s0 = jnp.zeros((b, h, dk, dv), f32)
    _, o = lax.scan(step, s0, (u_base, w, qk, q_dec, k_dec, g_tot))
    return from_chunks(o)


def hgrn2_recurrence(q, k, v, log_f):
    f32 = jnp.float32
    b, t, h, dk = q.shape
    dv = v.shape[-1]
    c = HG_CHUNK
    q, k, v, lf = (to_chunks(a.astype(f32), c) for a in (q, k, v, log_f))
    bcum = jnp.cumsum(lf, axis=3)
    causal = jnp.tril(jnp.ones((c, c), bool))[:, :, None]

    def step(state, inp):
        q_c, k_c, v_c, b_c = inp
        decay = masked_exp(causal, b_c[:, :, :, None, :] - b_c[:, :, None, :, :])
        attn = jnp.einsum('bhtsd,bhsd->bhts', decay * q_c[:, :, :, None, :], k_c)
        o = (jnp.einsum('bhts,bhse->bhte', attn, v_c)
             + jnp.einsum('bhtd,bhde->bhte', q_c * jnp.exp(b_c), state))
        b_last = b_c[:, :, -1:, :]
        state = (state * jnp.exp(b_last[:, :, 0, :, None])
                 + jnp.einsum('bhsd,bhse->bhde', k_c * jnp.exp(b_last - b_c), v_c))
        return state, o

    s0 = jnp.zeros((b, h, dk, dv), f32)
    _, o = lax.scan(step, s0, (q, k, v, bcum))
    return from_chunks(o)


def dilated_window_attention(q, k, v, window, dilation):
    f32 = jnp.float32
    b, t, h, dh = q.shape
    span = window // dilation
    length = t // dilation
    nb = -(-length // span)
    pad = nb * span - length

    def blocks(a):
        a = a.astype(f32).reshape(b, length, dilation, h, dh).transpose(0, 2, 3, 1, 4)
        a = jnp.pad(a, ((0, 0), (0, 0), (0, 0), (0, pad), (0, 0)))
        return a.reshape(b, dilation, h, nb, span, dh)

    def with_prev(a):
        prev = jnp.pad(a, ((0, 0), (0, 0), (0, 0), (1, 0), (0, 0), (0, 0)))[:, :, :, :-1]
        return jnp.concatenate([prev, a], axis=4)

    qb = blocks(q)
    kb = with_prev(blocks(k))
    vb = with_prev(blocks(v))
    s = jnp.einsum('brhnqd,brhnkd->brhnqk', qb, kb) * (dh ** -0.5)
    qi = jnp.arange(span)[:, None]
    kj = jnp.arange(2 * span)[None, :]
    dist = qi + span - kj
    in_range = jnp.arange(nb)[:, None, None] * span + kj - span >= 0
    valid = (dist >= 0) & (dist <= span) & in_range
    s = jnp.where(valid, s, MASK_VALUE)
    m = jnp.max(s, axis=-1, keepdims=True)
    p = jnp.where(valid, jnp.exp(s - m), 0.0)
    l = jnp.sum(p, axis=-1, keepdims=True)
    o = jnp.einsum('brhnqk,brhnkd->brhnqd', p / l, vb)
    lse = (m + jnp.log(l))[..., 0]
    o = o.reshape(b, dilation, h, nb * span, dh)[:, :, :, :length].transpose(0, 3, 1, 2, 4).reshape(b, t, h, dh)
    lse = lse.reshape(b, dilation, h, nb * span)[..., :length].transpose(0, 3, 1, 2).reshape(b, t, h)
    return o, lse


def mixer_ab(h, w_in, conv_w, a_log, dt_bias, norm_g, sc_conv_w, w_out):
    f32 = jnp.float32
    b, t, _ = h.shape
    qkv, z, beta_raw, alpha_raw, gate_b, gate_c, sc_in = split_cols(h @ w_in, AB_SPLITS)
    qkv = jax.nn.silu(causal_depthwise_conv(qkv, conv_w))
    q, k, v = split_cols(qkv, (DN_W, DN_W, DN_VW))
    q = l2_normalize(q.reshape(b, t, DN_HEADS, DN_DK)) * (DN_DK ** -0.5)
    k = l2_normalize(k.reshape(b, t, DN_HEADS, DN_DK))
    v = v.reshape(b, t, DN_HEADS, DN_DV)
    beta = jax.nn.sigmoid(beta_raw.astype(f32))
    g = -jnp.exp(a_log.astype(f32)) * jax.nn.softplus(alpha_raw.astype(f32) + dt_bias.astype(f32))
    o_a = gated_delta_rule(q, k, v, g, beta)
    o_a = (rms_norm(o_a, norm_g) * jax.nn.silu(z.reshape(b, t, DN_HEADS, DN_DV))).reshape(b, t, DN_VW)
    o_b = gate_b * causal_depthwise_conv(gate_c * sc_in, sc_conv_w)
    return jnp.concatenate([o_a.astype(h.dtype), o_b.astype(h.dtype)], axis=-1) @ w_out


def mixer_cd(h, w_in, lower_bound, norm_g, w_out):
    f32 = jnp.float32
    b, t, _ = h.shape
    hq, hf, hi, hg, sq, sk, sv = split_cols(h @ w_in, CD_SPLITS)
    lb = lower_bound.astype(f32)
    hf = hf.astype(f32)
    f_gate = lb + (1.0 - lb) * jax.nn.sigmoid(hf)
    log_f = jnp.log(f_gate)
    k_in = (1.0 - lb) * jax.nn.sigmoid(-hf)
    hgrn_heads = lambda a, d: a.reshape(b, t, HG_HEADS, d)
    o_c = hgrn2_recurrence(hgrn_heads(jax.nn.silu(hq), HG_DK), hgrn_heads(k_in, HG_DK),
                           hgrn_heads(hi, HG_DV), hgrn_heads(log_f, HG_DK))
    o_c = (rms_norm(o_c, norm_g) * jax.nn.sigmoid(hgrn_heads(hg, HG_DV).astype(f32))).reshape(b, t, HG_VW)
    n_pat = len(SWA_PATTERNS)
    pos = jnp.arange(t)
    sq = rope(sq.reshape(b, t, n_pat, SWA_HEADS, SWA_DH), pos)
    sk = rope(sk.reshape(b, t, n_pat, SWA_HEADS, SWA_DH), pos)
    sv = sv.reshape(b, t, n_pat, SWA_HEADS, SWA_DH)
    outs, lses = [], []
    for p, (window, dilation) in enumerate(SWA_PATTERNS):
        o_p, lse_p = dilated_window_attention(sq[:, :, p], sk[:, :, p], sv[:, :, p], window, dilation)
        outs.append(o_p)
        lses.append(lse_p)
    wts = jax.nn.softmax(jnp.stack(lses, axis=-1), axis=-1)
    o_d = jnp.einsum('bthpd,bthp->bthd', jnp.stack(outs, axis=3), wts).reshape(b, t, SWA_HEADS * SWA_DH)
    return jnp.concatenate([o_c.astype(h.dtype), o_d.astype(h.dtype)], axis=-1) @ w_out


def swiglu(h, w_gate, w_up, w_down):
    return (jax.nn.silu(h @ w_gate) * (h @ w_up)) @ w_down


def setup_inputs(seed: int = 0) -> dict:
    key = jax.random.key(seed)
    keys = iter(jax.random.split(key, 24))
    f32 = jnp.float32

    def dense(shape, fan_in):
        return jax.random.normal(next(keys), shape, f32) * (fan_in ** -0.5)

    def gain(shape):
        return 1.0 + 0.02 * jax.random.normal(next(keys), shape, f32)

    x = jax.random.normal(next(keys), (BATCH, SEQ, D_MODEL), f32)
    norm_mix_g = gain((DEPTH, D_MODEL))
    norm_ffn_g = gain((DEPTH, D_MODEL))
    norm_final_g = gain((D_MODEL,))
    ab_w_in = dense((N_EVEN, D_MODEL, AB_IN), D_MODEL)
    dn_conv_w = dense((N_EVEN, DN_CONV, 2 * DN_W + DN_VW), DN_CONV)
    dn_a_log = jnp.log(jax.random.uniform(next(keys), (N_EVEN, DN_HEADS), f32, 1.0, 16.0))
    dt = jnp.exp(jax.random.uniform(next(keys), (N_EVEN, DN_HEADS), f32, math.log(1e-3), math.log(1e-1)))
    dn_dt_bias = dt + jnp.log(-jnp.expm1(-dt))
    dn_norm_g = gain((N_EVEN, DN_DV))
    sc_conv_w = dense((N_EVEN, SC_CONV, SC_CHANNELS), SC_CONV)
    ab_w_out = dense((N_EVEN, AB_OUT, D_MODEL), AB_OUT)
    cd_w_in = dense((N_ODD, D_MODEL, CD_IN), D_MODEL)
    hg_lower_bounds = 0.1 * jax.random.normal(next(keys), (N_ODD, HG_W), f32)
    hg_norm_g = gain((N_ODD, HG_DV))
    cd_w_out = dense((N_ODD, CD_OUT, D_MODEL), CD_OUT)
    ffn_w_gate = dense((DEPTH, D_MODEL, FFN_HIDDEN), D_MODEL)
    ffn_w_up = dense((DEPTH, D_MODEL, FFN_HIDDEN), D_MODEL)
    ffn_w_down = dense((DEPTH, FFN_HIDDEN, D_MODEL), FFN_HIDDEN)
    return {'x': x, 'norm_mix_g': norm_mix_g, 'norm_ffn_g': norm_ffn_g, 'norm_final_g': norm_final_g,
            'ab_w_in': ab_w_in, 'dn_conv_w': dn_conv_w, 'dn_a_log': dn_a_log, 'dn_dt_bias': dn_dt_bias,
            'dn_norm_g': dn_norm_g, 'sc_conv_w': sc_conv_w, 'ab_w_out': ab_w_out,
            'cd_w_in': cd_w_in, 'hg_lower_bounds': hg_lower_bounds, 'hg_norm_g': hg_norm_g,
            'cd_w_out': cd_w_out, 'ffn_w_gate': ffn_w_gate, 'ffn_w_up': ffn_w_up, 'ffn_w_down': ffn_w_down}


def reference(x, norm_mix_g, norm_ffn_g, norm_final_g, ab_w_in, dn_conv_w, dn_a_log, dn_dt_bias,
              dn_norm_g, sc_conv_w, ab_w_out, cd_w_in, hg_lower_bounds, hg_norm_g, cd_w_out,
              ffn_w_gate, ffn_w_up, ffn_w_down):
    sm = jax.nn.softmax(hg_lower_bounds.astype(jnp.float32), axis=0)
    lower_bounds = jnp.cumsum(sm, axis=0) - sm[0]
    h = x
    for layer in range(DEPTH):
        i = layer // 2
        hn = rms_norm(h, norm_mix_g[layer])
        if layer % 2 == 0:
            h = h + mixer_ab(hn, ab_w_in[i], dn_conv_w[i], dn_a_log[i], dn_dt_bias[i], dn_norm_g[i],
                             sc_conv_w[i], ab_w_out[i])
        else:
            h = h + mixer_cd(hn, cd_w_in[i], lower_bounds[i], hg_norm_g[i], cd_w_out[i])
        hn = rms_norm(h, norm_ffn_g[layer])
        h = h + swiglu(hn, ffn_w_gate[layer], ffn_w_up[layer], ffn_w_down[layer])
    return rms_norm(h, norm_final_g)
```

```python
import numpy as np
from contextlib import ExitStack
import concourse.bass as bass
import concourse.mybir as mybir
from concourse.bass_utils import run_bass_kernel_spmd

F32 = mybir.dt.float32
BF16 = mybir.dt.bfloat16
AF = mybir.ActivationFunctionType
ALU = mybir.AluOpType

T = 4096
D = 1024
NT = T // 128
DEPTH = 4
FH = 2816
NFC = FH // 128
EPS = 1e-6
AB_IN = 3592
CD_IN = 6656
SAME_ENGINE_SYNC = True


class Buf:
    __slots__ = ("w", "r")

    def __init__(self):
        self.w = None
        self.r = []


def bufs(n):
    return [Buf() for _ in range(n)]


class KB:
    def __init__(self, nc, es):
        self.nc = nc
        self.es = es
        self.e = {"pe": nc.tensor, "act": nc.scalar, "dve": nc.vector, "pool": nc.gpsimd, "sp": nc.sync}
        self.sems = {}
        self.cnt = {}
        for k in self.e:
            self.sems[k] = es.enter_context(nc.semaphore("s_" + k))
            self.cnt[k] = 0
        self.seen = {k: {} for k in self.e}
        self.dq = {}
        self.dqi = {}
        for q, n in (("sp", 24), ("pool", 16), ("act", 8)):
            lst = []
            for i in range(n):
                key = "d_%s%d" % (q, i)
                self.sems[key] = es.enter_context(nc.semaphore(key))
                lst.append([key, 0])
            self.dq[q] = lst
            self.dqi[q] = 0
        self.n_inst = 0
        self.uid = 0

    def _wait(self, eng, tk):
        if tk is None:
            return
        key, val = tk
        if key == eng and (eng == "pe" or not SAME_ENGINE_SYNC):
            return
        if self.seen[eng].get(key, 0) >= val:
            return
        self.seen[eng][key] = val
        self.e[eng].wait_ge(self.sems[key], val)
        self.n_inst += 1

    def _deps(self, eng, R, W):
        for b in R:
            self._wait(eng, b.w)
        for b in W:
            self._wait(eng, b.w)
            for tk in b.r:
                self._wait(eng, tk)

    def op(self, eng, fn, R=(), W=()):
        return self._op(eng, fn, [getattr(x, "b", x) for x in R], [getattr(x, "b", x) for x in W])

    def dma(self, q, out, in_, R=(), W=()):
        return self._dma(q, out, in_, [getattr(x, "b", x) for x in R], [getattr(x, "b", x) for x in W])

    def _commit(self, tk, R, W):
        for b in R:
            b.r.append(tk)
            if len(b.r) > 64:
                b.r = b.r[-48:]
        for b in W:
            b.w = tk
            b.r = []

    def _op(self, eng, fn, R=(), W=()):
        self._deps(eng, R, W)
        ins = fn(self.e[eng])
        self.cnt[eng] += 1
        ins.then_inc(self.sems[eng], 1)
        tk = (eng, self.cnt[eng])
        self._commit(tk, R, W)
        self.n_inst += 1
        return tk

    def _dma(self, q, out, in_, R=(), W=()):
        lst = self.dq[q]
        i = self.dqi[q]
        self.dqi[q] = (i + 1) % len(lst)
        key, val = lst[i]
        if val:
            self._wait(q, (key, val))
        self._deps(q, R, W)
        ins = self.e[q].dma_start(out=out, in_=in_)
        val += 16
        lst[i][1] = val
        ins.then_inc(self.sems[key], 16)
        tk = (key, val)
        self._commit(tk, R, W)
        self.n_inst += 1
        return tk

    def barrier(self):
        tks = [(k, self.cnt[k]) for k in self.e if self.cnt[k]]
        for q in self.dq:
            for key, val in self.dq[q]:
                if val:
                    tks.append((key, val))
        for eng in self.e:
            for tk in tks:
                if tk[0] == eng:
                    continue
                self._wait(eng, tk)

    def finish(self):
        tks = []
        for q in self.dq:
            for key, val in self.dq[q]:
                if val:
                    tks.append((key, val))
        for tk in tks:
            self._wait("sp", tk)


class Scope:
    def __init__(self, kb):
        self.kb = kb
        self.es = ExitStack()

    def __enter__(self):
        self.es.__enter__()
        return self

    def __exit__(self, *a):
        self.kb.barrier()
        return self.es.__exit__(*a)

    def sb(self, name, shape, dt):
        self.kb.uid += 1
        return self.es.enter_context(self.kb.nc.sbuf_tensor("%s_u%d" % (name, self.kb.uid), list(shape), dt))

    def ps(self, name, shape, dt):
        self.kb.uid += 1
        return self.es.enter_context(self.kb.nc.psum_tensor("%s_u%d" % (name, self.kb.uid), list(shape), dt))


class TB:
    def __init__(self, t):
        self.t = t
        self.b = Buf()

    def __getitem__(self, idx):
        return self.t[idx]


def sbt(sc, name, shape, dt):
    return TB(sc.sb(name, shape, dt))


def TT(kb, eng, out, a, b, op, R, W):
    return kb.op(eng, lambda e: e.tensor_tensor(out=out, in0=a, in1=b, op=op), R, W)


def TS(kb, eng, out, a, s1, op0, R, W, s2=None, op1=None):
    if op1 is None:
        return kb.op(eng, lambda e: e.tensor_scalar(out=out, in0=a, scalar1=s1, scalar2=None, op0=op0), R, W)
    return kb.op(eng, lambda e: e.tensor_scalar(out=out, in0=a, scalar1=s1, scalar2=s2, op0=op0, op1=op1), R, W)


def STT(kb, eng, out, a, sc_, b, op0, op1, R, W):
    return kb.op(eng, lambda e: e.scalar_tensor_tensor(out=out, in0=a, scalar=sc_, in1=b, op0=op0, op1=op1), R, W)


def ACTF(kb, out, in_, func, R, W, **kw):
    return kb.op("act", lambda e: e.activation(out=out, in_=in_, func=func, **kw), R, W)


def MM(kb, out, lhsT, rhs, R, W, start=True, stop=True):
    return kb.op("pe", lambda e: e.matmul(out, lhsT=lhsT, rhs=rhs, start=start, stop=stop), R, W)


def TR(kb, out, in_, ident, R, W):
    return kb.op("pe", lambda e: e.transpose(out=out, in_=in_, identity=ident), R, W)


def h4(ap, n=128):
    return ap.rearrange("p (h n) -> p h n", h=4)


def bc4(ap, n=128):
    return ap.unsqueeze(2).to_broadcast([128, 4, n])


class PsumRing:
    def __init__(self, sc, name, n, shape=(128, 512), dt=F32):
        self.t = [sc.ps("%s%d" % (name, i), shape, dt) for i in range(n)]
        self.b = bufs(n)
        self.i = 0

    def next(self):
        i = self.i
        self.i = (i + 1) % len(self.t)
        return self.t[i], self.b[i]


def bc_row(ap1d, n):
    return ap1d.partition_broadcast(n)


def make_consts(kb, sc):
    c = {}
    c["idb"] = sc.sb("idb", [128, 128], BF16)
    c["idf"] = sc.sb("idf", [128, 128], F32)
    c["onesf"] = sc.sb("onesf", [128, 128], F32)
    c["b_id"] = Buf()
    kb.op("pool", lambda e: e.memset(c["onesf"][:], 1.0), W=[c["b_id"]])
    kb.op("pool", lambda e: e.memset(c["idb"][:], 1.0), W=[c["b_id"]])
    kb.op("pool", lambda e: e.affine_select(out=c["idb"][:], in_=c["idb"][:], pattern=[[-1, 128]],
                                            compare_op=ALU.is_equal, fill=0.0, base=0, channel_multiplier=1),
          R=[c["b_id"]], W=[c["b_id"]])
    kb.op("pool", lambda e: e.memset(c["idf"][:], 1.0), W=[c["b_id"]])
    kb.op("pool", lambda e: e.affine_select(out=c["idf"][:], in_=c["idf"][:], pattern=[[-1, 128]],
                                            compare_op=ALU.is_equal, fill=0.0, base=0, channel_multiplier=1),
          R=[c["b_id"]], W=[c["b_id"]])
    return c


def norm_transpose(kb, sc, c, h_src, g_row, hnT, hnT_b, psr, tag):
    gb = sc.sb(tag + "_g", [128, D], F32)
    gb_b = Buf()
    kb.dma("sp", gb[:], bc_row(g_row, 128), W=[gb_b])
    ht = [sc.sb(tag + "_h%d" % i, [128, D], F32) for i in range(2)]
    ht_b = bufs(2)
    junk = sc.sb(tag + "_junk", [128, D], BF16)
    junk_b = Buf()
    hn = [sc.sb(tag + "_hn%d" % i, [128, D], BF16) for i in range(2)]
    hn_b = bufs(2)
    st = sc.sb(tag + "_st", [128, NT, 4], F32)
    st_b = bufs(NT)
    for i in range(NT):
        s = i % 2
        kb.dma("sp", ht[s][:], h_src[i * 128:(i + 1) * 128, :], R=[HB(h_src, i)], W=[ht_b[s]])
        kb.op("act", lambda e: e.activation(out=junk[:], in_=ht[s][:], func=AF.Square, scale=float(D ** -0.5),
                                            accum_out=st[:, i, 0:1]), R=[ht_b[s]], W=[junk_b, st_b[i]])
        kb.op("act", lambda e: e.activation(out=st[:, i, 1:2], in_=st[:, i, 0:1], func=AF.Sqrt, bias=EPS_AP(kb), scale=1.0),
              R=[st_b[i]], W=[st_b[i]])
        kb.op("dve", lambda e: e.reciprocal(out=st[:, i, 2:3], in_=st[:, i, 1:2]), R=[st_b[i]], W=[st_b[i]])
        kb.op("dve", lambda e: e.scalar_tensor_tensor(out=hn[s][:], in0=ht[s][:], scalar=st[:, i, 2:3], in1=gb[:],
                                                      op0=ALU.mult, op1=ALU.mult),
              R=[ht_b[s], st_b[i], gb_b], W=[hn_b[s]])
        pt, pb = psr.next()
        for dc in range(8):
            kb.op("pe", lambda e: e.transpose(out=pt[:, dc * 128:(dc + 1) * 128], in_=hn[s][:, dc * 128:(dc + 1) * 128],
                                              identity=c["idb"][:]), R=[hn_b[s], c["b_id"]], W=[pb])
        kb.op("act", lambda e: e.copy(out=hnT[:, :, i * 128:(i + 1) * 128],
                                      in_=pt[:, :].rearrange("p (c n) -> p c n", c=8)), R=[pb], W=[hnT_b[i]])


_EPS_AP = {}


def EPS_AP(kb):
    return _EPS_AP["ap"]


def make_eps(kb, sc):
    t = sc.sb("eps_t", [128, 1], F32)
    b = Buf()
    kb.op("pool", lambda e: e.memset(t[:], EPS), W=[b])
    kb.barrier()
    _EPS_AP["ap"] = t[:, 0:1]


def load_w_cols(kb, wdst, wdst_b, w_dram, c0, cw, q="pool"):
    src = w_dram[:, c0:c0 + cw].rearrange("(c p) n -> p c n", p=128)
    kb.dma(q, wdst, src, W=[wdst_b])


def ffn_layer(kb, c, h_src, h_dst, g_row, wg, wu, wd, actT_d, lname):
    actT_b = [bufs(8) for _ in range(NFC)]
    with Scope(kb) as sc:
        hnT = sc.sb(lname + "hnT", [128, 8, T], BF16)
        hnT_b = bufs(NT)
        psr_t = PsumRing(sc, lname + "pt", 2, (128, 1024), BF16)
        norm_transpose(kb, sc, c, h_src, g_row, hnT, hnT_b, psr_t, lname + "n")
        psr = PsumRing(sc, lname + "pp", 6)
        wgt = [sc.sb(lname + "wg%d" % i, [128, 8, 128], BF16) for i in range(2)]
        wut = [sc.sb(lname + "wu%d" % i, [128, 8, 128], BF16) for i in range(2)]
        wg_b = bufs(2)
        wu_b = bufs(2)
        sg = [sc.sb(lname + "sg%d" % i, [128, 512], F32) for i in range(2)]
        sg_b = bufs(2)
        arow = [sc.sb(lname + "ar%d" % i, [128, T], BF16) for i in range(2)]
        arow_b = bufs(2)
        k = 0
        for n in range(NFC):
            s = n % 2
            load_w_cols(kb, wgt[s][:], wg_b[s], wg, n * 128, 128)
            load_w_cols(kb, wut[s][:], wu_b[s], wu, n * 128, 128)
            for tt in range(8):
                pg, pgb = psr.next()
                pu, pub = psr.next()
                hb = hnT_b[tt * 4:(tt + 1) * 4]
                for dc in range(8):
                    kb.op("pe", lambda e: e.matmul(pg[:], lhsT=wgt[s][:, dc, :], rhs=hnT[:, dc, tt * 512:(tt + 1) * 512],
                                                   start=(dc == 0), stop=(dc == 7)), R=[wg_b[s]] + hb, W=[pgb])
                for dc in range(8):
                    kb.op("pe", lambda e: e.matmul(pu[:], lhsT=wut[s][:, dc, :], rhs=hnT[:, dc, tt * 512:(tt + 1) * 512],
                                                   start=(dc == 0), stop=(dc == 7)), R=[wu_b[s]] + hb, W=[pub])
                ss = k % 2
                k += 1
                kb.op("act", lambda e: e.activation(out=sg[ss][:], in_=pg[:], func=AF.Silu), R=[pgb], W=[sg_b[ss]])
                kb.op("dve", lambda e: e.tensor_tensor(out=arow[s][:, tt * 512:(tt + 1) * 512], in0=sg[ss][:], in1=pu[:],
                                                       op=ALU.mult), R=[sg_b[ss], pub], W=[arow_b[s]])
            kb.dma("sp", actT_d[n, :, :], arow[s][:], R=[arow_b[s]], W=[b for b in actT_b[n]])
    with Scope(kb) as sc:
        wdt = sc.sb(lname + "wd", [128, NFC, D], BF16)
        wd_b = Buf()
        for n0 in range(0, NFC, 2):
            kb.dma("pool", wdt[:, n0:n0 + 2, :], wd[n0 * 128:(n0 + 2) * 128, :].rearrange("(c p) n -> p c n", p=128), W=[wd_b])
        proj_tm_residual(kb, sc, actT_d, actT_b, NFC, wdt, wd_b, h_src, h_dst, lname + "d")


def proj_tm_residual(kb, sc, xT_d, xT_b, nk, wt, wt_b, h_src, h_dst, tag):
    psr = PsumRing(sc, tag + "ps", 4)
    xt = [sc.sb(tag + "x%d" % i, [128, nk, 512], BF16) for i in range(2)]
    xt_b = bufs(2)
    ht = [sc.sb(tag + "h%d" % i, [128, D], F32) for i in range(3)]
    ht_b = bufs(3)
    j = 0
    for tt in range(8):
        s = tt % 2
        kb.dma("sp", xt[s][:], xT_d[:, :, tt * 512:(tt + 1) * 512].rearrange("k p n -> p k n"),
               R=[xT_b[k][tt] for k in range(nk)], W=[xt_b[s]])
        for sub in range(4):
            i = tt * 4 + sub
            hs = j % 3
            j += 1
            kb.dma("sp", ht[hs][:], h_src[i * 128:(i + 1) * 128, :], R=[HB(h_src, i)], W=[ht_b[hs]])
            for half in range(2):
                pt, pb = psr.next()
                for k in range(nk):
                    kb.op("pe", lambda e: e.matmul(pt[:], lhsT=xt[s][:, k, sub * 128:(sub + 1) * 128],
                                                   rhs=wt[:, k, half * 512:(half + 1) * 512],
                                                   start=(k == 0), stop=(k == nk - 1)), R=[xt_b[s], wt_b], W=[pb])
                kb.op("dve", lambda e: e.tensor_tensor(out=ht[hs][:, half * 512:(half + 1) * 512],
                                                       in0=ht[hs][:, half * 512:(half + 1) * 512], in1=pt[:], op=ALU.add),
                      R=[pb, ht_b[hs]], W=[ht_b[hs]])
            kb.dma("sp", h_dst[i * 128:(i + 1) * 128, :], ht[hs][:], R=[ht_b[hs]], W=[HB(h_dst, i)])


_HB = {}


def HB(ap, i):
    key = ap.tensor.name
    if key not in _HB:
        _HB[key] = bufs(NT)
    return _HB[key][i]


def final_norm(kb, h_src, g_row, out):
    with Scope(kb) as sc:
        gb = sc.sb("fn_g", [128, D], F32)
        gb_b = Buf()
        kb.dma("sp", gb[:], bc_row(g_row, 128), W=[gb_b])
        ht = [sc.sb("fn_h%d" % i, [128, D], F32) for i in range(3)]
        ht_b = bufs(3)
        junk = sc.sb("fn_junk", [128, D], BF16)
        junk_b = Buf()
        st = sc.sb("fn_st", [128, NT, 4], F32)
        st_b = bufs(NT)
        for i in range(NT):
            s = i % 3
            kb.dma("sp", ht[s][:], h_src[i * 128:(i + 1) * 128, :], R=[HB(h_src, i)], W=[ht_b[s]])
            kb.op("act", lambda e: e.activation(out=junk[:], in_=ht[s][:], func=AF.Square, scale=float(D ** -0.5),
                                                accum_out=st[:, i, 0:1]), R=[ht_b[s]], W=[junk_b, st_b[i]])
            kb.op("act", lambda e: e.activation(out=st[:, i, 1:2], in_=st[:, i, 0:1], func=AF.Sqrt, bias=EPS_AP(kb), scale=1.0),
                  R=[st_b[i]], W=[st_b[i]])
            kb.op("dve", lambda e: e.reciprocal(out=st[:, i, 2:3], in_=st[:, i, 1:2]), R=[st_b[i]], W=[st_b[i]])
            kb.op("dve", lambda e: e.scalar_tensor_tensor(out=ht[s][:], in0=ht[s][:], scalar=st[:, i, 2:3], in1=gb[:],
                                                          op0=ALU.mult, op1=ALU.mult),
                  R=[ht_b[s], st_b[i], gb_b], W=[ht_b[s]])
            kb.dma("sp", out[i * 128:(i + 1) * 128, :], ht[s][:], R=[ht_b[s]], W=[HB(out, i)])


def proj_fm_chunk(kb, hnT, hnT_b, wt, psr, epi):
    for tt in range(8):
        pt, pb = psr.next()
        hb = hnT_b[tt * 4:(tt + 1) * 4]
        for dc in range(8):
            MM(kb, pt[:], wt[:, dc, :], hnT[:, dc, tt * 512:(tt + 1) * 512], R=[wt] + hb, W=[pb], start=(dc == 0), stop=(dc == 7))
        epi(tt, pt, pb)


def affsel(kb, t, pattern, cm, op, fill, W):
    kb.op("pool", lambda e: e.affine_select(out=t, in_=t, pattern=pattern, compare_op=op, fill=fill, base=0,
                                            channel_multiplier=cm), R=W, W=W)


def ab_layer(kb, c, P, i, layer, h_src, h_dst, S):
    w_in = P["ab_w_in"][i]
    qkvT_d = [S["qT_d"], S["kT_d"], S["vT_d"]]
    qkvT_b = [bufs(4) for _ in range(3)]
    z_d = S["z_d"]
    z_b = bufs(NT)
    oT_d = S["oT_d"]
    oT_b = [bufs(8) for _ in range(8)]
    with Scope(kb) as lsc:
        ba = sbt(lsc, "ab_ba", [128, NT, 8], F32)
        with Scope(kb) as sc:
            hnT = sc.sb("ab_hnT", [128, 8, T], BF16)
            hnT_b = bufs(NT)
            psr_t = PsumRing(sc, "ab_pt", 2, (128, 1024), BF16)
            norm_transpose(kb, sc, c, h_src, P["norm_mix_g"][layer], hnT, hnT_b, psr_t, "abn")
            psr = PsumRing(sc, "ab_pp", 6)
            wts = [sbt(sc, "ab_w%d" % j, [128, 8, 128], BF16) for j in range(2)]
            wi = [0]

            def next_w(c0):
                w = wts[wi[0] % 2]
                wi[0] += 1
                load_w_cols(kb, w[:], w, w_in, c0, 128)
                return w

            cw = sbt(sc, "ab_cw", [128, 12, 4], F32)
            kb.dma("sp", cw[:], P["dn_conv_wT"][i].rearrange("(c p) j -> p c j", p=128), W=[cw])
            scw = sbt(sc, "ab_scw", [128, 4, 3], F32)
            kb.dma("sp", scw[:], P["sc_conv_wT"][i].rearrange("(c p) j -> p c j", p=128), W=[scw])
            xpre = [sbt(sc, "ab_xp%d" % j, [128, 3 + T], F32) for j in range(2)]
            for xp in xpre:
                kb.op("pool", lambda e: e.memset(xp[:, 0:3], 0.0), W=[xp])
            acc = sbt(sc, "ab_acc", [128, T], F32)
            y = sbt(sc, "ab_y", [128, T], F32)
            orow = [sbt(sc, "ab_or%d" % j, [128, T], BF16) for j in range(2)]
            rn = [sbt(sc, "ab_rn%d" % j, [128, 512], F32) for j in range(2)]
            for j in range(12):
                xp = xpre[j % 2]
                w = next_w(j * 128)
                proj_fm_chunk(kb, hnT, hnT_b, w, psr,
                              lambda tt, pt, pb: ACTF(kb, xp[:, 3 + tt * 512:3 + (tt + 1) * 512], pt[:], AF.Copy, R=[pb], W=[xp]))
                TS(kb, "dve", acc[:], xp[:, 0:T], cw[:, j, 0:1], ALU.mult, R=[xp, cw], W=[acc])
                for t in range(1, 4):
                    STT(kb, "dve", acc[:], xp[:, t:t + T], cw[:, j, t:t + 1], acc[:], ALU.mult, ALU.add, R=[xp, cw, acc], W=[acc])
                ACTF(kb, y[:], acc[:], AF.Silu, R=[acc], W=[y])
                orw = orow[j % 2]
                if j < 8:
                    ACTF(kb, acc[:], y[:], AF.Square, R=[y], W=[acc])
                    qs = float(128 ** -0.5) if j < 4 else 1.0
                    for tt in range(8):
                        pt, pb = psr.next()
                        MM(kb, pt[:], c["onesf"][:], acc[:, tt * 512:(tt + 1) * 512], R=[acc, c["b_id"]], W=[pb])
                        r = rn[tt % 2]
                        ACTF(kb, r[:], pt[:], AF.Sqrt, R=[pb], W=[r], bias=_EPS_AP["ap"], scale=1.0)
                        kb.op("dve", lambda e: e.reciprocal(out=r[:], in_=r[:]), R=[r], W=[r])
                        STT(kb, "dve", orw[:, tt * 512:(tt + 1) * 512], y[:, tt * 512:(tt + 1) * 512], qs, r[:],
                            ALU.mult, ALU.mult, R=[y, r], W=[orw])
                else:
                    kb.op("dve", lambda e: e.tensor_copy(out=orw[:], in_=y[:]), R=[y], W=[orw])
                kb.dma("sp", qkvT_d[j // 4][j % 4, :, :], orw[:], R=[orw], W=[qkvT_b[j // 4][j % 4]])
            for j in range(4):
                xp = xpre[j % 2]
                orw = orow[j % 2]
                w = next_w(2568 + j * 128)
                proj_fm_chunk(kb, hnT, hnT_b, w, psr,
                              lambda tt, pt, pb: ACTF(kb, y[:, tt * 512:(tt + 1) * 512], pt[:], AF.Copy, R=[pb], W=[y]))
                w = next_w(3080 + j * 128)
                proj_fm_chunk(kb, hnT, hnT_b, w, psr,
                              lambda tt, pt, pb: TT(kb, "dve", xp[:, 3 + tt * 512:3 + (tt + 1) * 512], pt[:],
                                                    y[:, tt * 512:(tt + 1) * 512], ALU.mult, R=[pb, y], W=[xp]))
                TS(kb, "dve", acc[:], xp[:, 1:1 + T], scw[:, j, 0:1], ALU.mult, R=[xp, scw], W=[acc])
                for t in range(1, 3):
                    STT(kb, "dve", acc[:], xp[:, 1 + t:1 + t + T], scw[:, j, t:t + 1], acc[:], ALU.mult, ALU.add,
                        R=[xp, scw, acc], W=[acc])
                w = next_w(2056 + j * 128)
                proj_fm_chunk(kb, hnT, hnT_b, w, psr,
                              lambda tt, pt, pb: TT(kb, "dve", orw[:, tt * 512:(tt + 1) * 512], pt[:],
                                                    acc[:, tt * 512:(tt + 1) * 512], ALU.mult, R=[pb, acc], W=[orw]))
                kb.dma("sp", oT_d[4 + j, :, :], orw[:], R=[orw], W=oT_b[4 + j])
            wz = sbt(sc, "ab_wz", [128, 8, 512], BF16)
            load_w_cols(kb, wz[:], wz, w_in, 1536, 512)
            zts = [sbt(sc, "ab_zt%d" % j, [128, 512], BF16) for j in range(2)]
            for it in range(NT):
                pt, pb = psr.next()
                for dc in range(8):
                    MM(kb, pt[:], hnT[:, dc, it * 128:(it + 1) * 128], wz[:, dc, :], R=[hnT_b[it], wz], W=[pb],
                       start=(dc == 0), stop=(dc == 7))
                zt = zts[it % 2]
                ACTF(kb, zt[:], pt[:], AF.Silu, R=[pb], W=[zt])
                kb.dma("sp", z_d[it * 128:(it + 1) * 128, :], zt[:], R=[zt], W=[z_b[it]])
            wba = sbt(sc, "ab_wba", [128, 8, 8], BF16)
            load_w_cols(kb, wba[:], wba, w_in, 2048, 8)
            pt, pb = psr.next()
            for it in range(NT):
                for dc in range(8):
                    MM(kb, pt[:, it * 8:(it + 1) * 8], hnT[:, dc, it * 128:(it + 1) * 128], wba[:, dc, :], R=[hnT_b[it], wba],
                       W=[pb], start=(dc == 0), stop=(dc == 7))
            kb.op("dve", lambda e: e.tensor_copy(out=ba[:], in_=pt[:, 0:NT * 8].rearrange("p (n c) -> p n c", c=8)),
                  R=[pb], W=[ba])
        with Scope(kb) as sc:
            psr = PsumRing(sc, "gd_p", 6)
            psb = PsumRing(sc, "gd_pb", 2, (128, 1024), BF16)
            tri = sbt(sc, "gd_tri", [128, 128], F32)
            kb.op("pool", lambda e: e.memset(tri[:], 1.0), W=[tri])
            affsel(kb, tri[:], [[1, 128]], -1, ALU.is_ge, 0.0, [tri])
            NM = sbt(sc, "gd_NM", [128, 4, 128], F32)
            kb.op("pool", lambda e: e.memset(NM[:], 0.0), W=[NM])
            affsel(kb, NM[:], [[0, 4], [1, 128]], -1, ALU.is_ge, -1.0e4, [NM])
            SM = sbt(sc, "gd_SM", [128, 4, 128], F32)
            kb.op("pool", lambda e: e.memset(SM[:], 1.0), W=[SM])
            affsel(kb, SM[:], [[0, 4], [1, 128]], -1, ALU.is_gt, 0.0, [SM])
            I4 = sbt(sc, "gd_I4", [128, 4, 128], F32)
            kb.op("pool", lambda e: e.memset(I4[:], 1.0), W=[I4])
            affsel(kb, I4[:], [[0, 4], [1, 128]], -1, ALU.is_equal, 0.0, [I4])
            dtb = sbt(sc, "gd_dtb", [128, 4], F32)
            kb.dma("sp", dtb[:], bc_row(P["dn_dt_bias"][i], 128), W=[dtb])
            nA = sbt(sc, "gd_nA", [128, 4], F32)
            kb.dma("sp", nA[:], bc_row(P["dn_a_log"][i], 128), W=[nA])
            ACTF(kb, nA[:], nA[:], AF.Exp, R=[nA], W=[nA])
            TS(kb, "dve", nA[:], nA[:], -1.0, ALU.mult, R=[nA], W=[nA])
            ng = sbt(sc, "gd_ng", [128, 128], F32)
            kb.dma("sp", ng[:], bc_row(P["dn_norm_g"][i], 128), W=[ng])
            beta = sbt(sc, "gd_beta", [128, NT, 4], F32)
            ACTF(kb, beta[:], ba[:, :, 0:4], AF.Sigmoid, R=[ba], W=[beta])
            nbeta = sbt(sc, "gd_nbeta", [128, NT, 4], F32)
            TS(kb, "dve", nbeta[:], beta[:], -1.0, ALU.mult, R=[beta], W=[nbeta])
            g = sbt(sc, "gd_g", [128, NT, 4], F32)
            TT(kb, "dve", g[:], ba[:, :, 4:8], dtb[:].unsqueeze(1).to_broadcast([128, NT, 4]), ALU.add, R=[ba, dtb], W=[g])
            ACTF(kb, g[:], g[:], AF.Exp, R=[g], W=[g])
            TS(kb, "dve", g[:], g[:], 1.0, ALU.add, R=[g], W=[g])
            ACTF(kb, g[:], g[:], AF.Ln, R=[g], W=[g])
            TT(kb, "dve", g[:], g[:], nA[:].unsqueeze(1).to_broadcast([128, NT, 4]), ALU.mult, R=[g, nA], W=[g])

            def f4(name, dt=F32):
                return sbt(sc, "gd_" + name, [128, 4, 128], dt)

            qkvs = [sbt(sc, "gd_qkv%d" % j, [128, 3, 4, 128], BF16) for j in range(2)]
            zts = [sbt(sc, "gd_zt%d" % j, [128, 512], BF16) for j in range(2)]
            Grep, dm, Dt, A2, P0T, P0 = f4("Grep"), f4("dm"), f4("Dt"), f4("A2"), f4("P0T"), f4("P0")
            Pb = [f4("Pa"), f4("Pb")]
            PTb = [f4("PTa"), f4("PTb")]
            XTb = [f4("XTa"), f4("XTb")]
            XTbf = f4("XTbf", BF16)
            tmp, o1, o, on, zg, junk = f4("tmp"), f4("o1"), f4("o"), f4("on"), f4("zg"), f4("junk")
            kd, vtm, qkT, Tm, u, of = f4("kd", BF16), f4("vtm", BF16), f4("qkT", BF16), f4("Tm", BF16), f4("u", BF16), f4("of", BF16)
            St = f4("S")
            Sbf = f4("Sbf", BF16)
            kb.op("pool", lambda e: e.memset(St[:], 0.0), W=[St])
            kb.op("pool", lambda e: e.memset(Sbf[:], 0.0), W=[Sbf])
            sm = sbt(sc, "gd_sm", [128, 8, 4], F32)
            ss = sbt(sc, "gd_ss", [128, 12], F32)
            oaT = sbt(sc, "gd_oaT", [128, 4, T], BF16)
            idf, idb = c["idf"], c["idb"]
            for n in range(NT):
                cs = slice(n * 128, (n + 1) * 128)
                qkv = qkvs[n % 2]
                zt = zts[n % 2]
                for a in range(3):
                    kb.dma("sp", qkv[:, a, :, :], qkvT_d[a][:, :, cs].rearrange("h p n -> p h n"), R=qkvT_b[a], W=[qkv])
                kb.dma("sp", zt[:], z_d[cs, :], R=[z_b[n]], W=[zt])
                qT = lambda hd: qkv[:, 0, hd, :]
                kT = lambda hd: qkv[:, 1, hd, :]
                kb.op("pool", lambda e: e.tensor_copy(out=Grep[:], in_=bc4(g[:, n, :])), R=[g], W=[Grep])
                pgr, pgrb = psr.next()
                for hd in range(4):
                    MM(kb, pgr[:, hd * 128:(hd + 1) * 128], Grep[:, hd, :], tri[:], R=[Grep, tri], W=[pgrb])
                pgc, pgcb = psr.next()
                MM(kb, pgc[:, 0:4], tri[:], g[:, n, :], R=[tri, g], W=[pgcb])
                ACTF(kb, sm[:, 0, :], pgc[:, 0:4], AF.Copy, R=[pgcb], W=[sm])
                ACTF(kb, sm[:, 1, :], pgc[:, 0:4], AF.Exp, R=[pgcb], W=[sm])
                TS(kb, "dve", sm[:, 2, :], sm[:, 1, :], -1.0, ALU.mult, R=[sm], W=[sm])
                kb.op("dve", lambda e: e.tensor_copy(out=sm[:, 3, :], in_=h4(pgr[:, :])[:, :, 127]), R=[pgrb], W=[sm])
                ACTF(kb, sm[:, 4, :], sm[:, 3, :], AF.Exp, R=[sm], W=[sm])
                TT(kb, "dve", sm[:, 5, :], sm[:, 3, :], sm[:, 0, :], ALU.subtract, R=[sm], W=[sm])
                ACTF(kb, sm[:, 5, :], sm[:, 5, :], AF.Exp, R=[sm], W=[sm])
                for hd in range(4):
                    STT(kb, "dve", dm[:, hd, :], pgr[:, hd * 128:(hd + 1) * 128], sm[:, 0, hd:hd + 1], NM[:, hd, :],
                        ALU.subtract, ALU.add, R=[pgrb, sm, NM], W=[dm])
                ACTF(kb, Dt[:], dm[:], AF.Exp, R=[dm], W=[Dt])
                pkv, pkvb = psb.next()
                for hd in range(4):
                    TR(kb, pkv[:, hd * 128:(hd + 1) * 128], qkv[:, 1, hd, :], idb[:], R=[qkv, c["b_id"]], W=[pkvb])
                for hd in range(4):
                    TR(kb, pkv[:, 512 + hd * 128:512 + (hd + 1) * 128], qkv[:, 2, hd, :], idb[:], R=[qkv, c["b_id"]], W=[pkvb])
                TT(kb, "dve", kd[:], h4(pkv[:, 0:512]), bc4(sm[:, 5, :]), ALU.mult, R=[pkvb, sm], W=[kd])
                ACTF(kb, vtm[:], h4(pkv[:, 512:1024]), AF.Copy, R=[pkvb], W=[vtm])
                pM, pMb = psr.next()
                for hd in range(4):
                    MM(kb, pM[:, hd * 128:(hd + 1) * 128], kT(hd), kT(hd), R=[qkv], W=[pMb])
                pQ, pQb = psr.next()
                for hd in range(4):
                    MM(kb, pQ[:, hd * 128:(hd + 1) * 128], kT(hd), qT(hd), R=[qkv], W=[pQb])
                TT(kb, "dve", qkT[:], h4(pQ[:, :]), Dt[:], ALU.mult, R=[pQb, Dt], W=[qkT])
                TT(kb, "pool", A2[:], Dt[:], SM[:], ALU.mult, R=[Dt, SM], W=[A2])
                TT(kb, "pool", A2[:], A2[:], bc4(nbeta[:, n, :]), ALU.mult, R=[A2, nbeta], W=[A2])
                TT(kb, "dve", P0T[:], h4(pM[:, :]), A2[:], ALU.mult, R=[pMb, A2], W=[P0T])
                pT, pTb = psr.next()
                for hd in range(4):
                    TR(kb, pT[:, hd * 128:(hd + 1) * 128], P0T[:, hd, :], idf[:], R=[P0T, c["b_id"]], W=[pTb])
                ACTF(kb, P0[:], h4(pT[:, :]), AF.Copy, R=[pTb], W=[P0])
                XT = XTb[0]
                TT(kb, "pool", XT[:], P0T[:], I4[:], ALU.add, R=[P0T, I4], W=[XT])
                Pa, PaT = P0, P0T
                for k in range(1, 7):
                    Pn = Pb[k % 2]
                    pP, pPb = psr.next()
                    for hd in range(4):
                        MM(kb, pP[:, hd * 128:(hd + 1) * 128], PaT[:, hd, :], Pa[:, hd, :], R=[PaT, Pa], W=[pPb])
                    ACTF(kb, Pn[:], h4(pP[:, :]), AF.Copy, R=[pPb], W=[Pn])
                    PnT = None
                    if k < 6:
                        PnT = PTb[k % 2]
                        pP2, pP2b = psr.next()
                        for hd in range(4):
                            MM(kb, pP2[:, hd * 128:(hd + 1) * 128], Pa[:, hd, :], PaT[:, hd, :], R=[PaT, Pa], W=[pP2b])
                        kb.op("dve", lambda e: e.tensor_copy(out=PnT[:], in_=h4(pP2[:, :])), R=[pP2b], W=[PnT])
                    pX, pXb = psr.next()
                    for hd in range(4):
                        MM(kb, pX[:, hd * 128:(hd + 1) * 128], Pn[:, hd, :], XT[:, hd, :], R=[Pn, XT], W=[pXb])
                    XTn = XTbf if k == 6 else XTb[k % 2]
                    TT(kb, "dve", XTn[:], XT[:], h4(pX[:, :]), ALU.add, R=[XT, pXb], W=[XTn])
                    Pa, PaT, XT = Pn, PnT, XTn
                pKS, pKSb = psr.next()
                for hd in range(4):
                    MM(kb, pKS[:, hd * 128:(hd + 1) * 128], kT(hd), Sbf[:, hd, :], R=[qkv, Sbf], W=[pKSb])
                TT(kb, "dve", tmp[:], h4(pKS[:, :]), bc4(sm[:, 2, :]), ALU.mult, R=[pKSb, sm], W=[tmp])
                TT(kb, "pool", Tm[:], tmp[:], vtm[:], ALU.add, R=[tmp, vtm], W=[Tm])
                pU, pUb = psr.next()
                for hd in range(4):
                    MM(kb, pU[:, hd * 128:(hd + 1) * 128], XTbf[:, hd, :], Tm[:, hd, :], R=[XTbf, Tm], W=[pUb])
                TT(kb, "dve", u[:], h4(pU[:, :]), bc4(beta[:, n, :]), ALU.mult, R=[pUb, beta], W=[u])
                pO1, pO1b = psr.next()
                for hd in range(4):
                    MM(kb, pO1[:, hd * 128:(hd + 1) * 128], qT(hd), Sbf[:, hd, :], R=[qkv, Sbf], W=[pO1b])
                pO2, pO2b = psr.next()
                for hd in range(4):
                    MM(kb, pO2[:, hd * 128:(hd + 1) * 128], qkT[:, hd, :], u[:, hd, :], R=[qkT, u], W=[pO2b])
                TT(kb, "dve", o1[:], h4(pO1[:, :]), bc4(sm[:, 1, :]), ALU.mult, R=[pO1b, sm], W=[o1])
                TT(kb, "dve", o[:], o1[:], h4(pO2[:, :]), ALU.add, R=[o1, pO2b], W=[o])
                pdS, pdSb = psr.next()
                for hd in range(4):
                    MM(kb, pdS[:, hd * 128:(hd + 1) * 128], kd[:, hd, :], u[:, hd, :], R=[kd, u], W=[pdSb])
                for hd in range(4):
                    STT(kb, "dve", St[:, hd, :], St[:, hd, :], sm[:, 4, hd:hd + 1], pdS[:, hd * 128:(hd + 1) * 128],
                        ALU.mult, ALU.add, R=[St, sm, pdSb], W=[St])
                ACTF(kb, Sbf[:], St[:], AF.Copy, R=[St], W=[Sbf])
                for hd in range(4):
                    ACTF(kb, junk[:, hd, :], o[:, hd, :], AF.Square, R=[o], W=[junk, ss], accum_out=ss[:, hd:hd + 1])
                ACTF(kb, ss[:, 4:8], ss[:, 0:4], AF.Sqrt, R=[ss], W=[ss], bias=_EPS_AP["ap"], scale=1.0 / 128)
                kb.op("dve", lambda e: e.reciprocal(out=ss[:, 8:12], in_=ss[:, 4:8]), R=[ss], W=[ss])
                TT(kb, "dve", on[:], o[:], bc4(ss[:, 8:12]), ALU.mult, R=[o, ss], W=[on])
                TT(kb, "pool", zg[:], h4(zt[:]), ng[:].unsqueeze(1).to_broadcast([128, 4, 128]), ALU.mult, R=[zt, ng], W=[zg])
                TT(kb, "pool", of[:], on[:], zg[:], ALU.mult, R=[on, zg], W=[of])
                pF, pFb = psb.next()
                for hd in range(4):
                    TR(kb, pF[:, hd * 128:(hd + 1) * 128], of[:, hd, :], idb[:], R=[of, c["b_id"]], W=[pFb])
                ACTF(kb, oaT[:, :, cs], h4(pF[:, 0:512]), AF.Copy, R=[pFb], W=[oaT])
            for hd in range(4):
                kb.dma("sp", oT_d[hd, :, :], oaT[:, hd, :], R=[oaT], W=oT_b[hd])
        with Scope(kb) as sc:
            wo = sbt(sc, "ab_wo", [128, 8, D], BF16)
            for n0 in range(0, 8, 2):
                kb.dma("pool", wo[:, n0:n0 + 2, :], P["ab_w_out"][i][n0 * 128:(n0 + 2) * 128, :].rearrange("(c p) n -> p c n", p=128), W=[wo])
            proj_tm_residual(kb, sc, oT_d, oT_b, 8, wo.t, wo.b, h_src, h_dst, "abo")


PATS = ((128, 1), (512, 4), (2048, 16))


def perm_cols(ap2d, d, u0, n):
    L = T // d
    v = ap2d.rearrange("p (j r) -> p r j", r=d)
    r0, j0 = u0 // L, u0 % L
    if j0 + n <= L:
        return v[:, r0, j0:j0 + n]
    nr = n // L
    return v[:, r0:r0 + nr, :]


def cd_layer(kb, c, P, i, layer, h_src, h_dst, S):
    w_in = P["cd_w_in"][i]
    qeT_d, keT_d, v_d, gt_d, oT_d = S["qT_d"], S["kT_d"], S["v_d"], S["z_d"], S["oT_d"]
    qe_b, ke_b = bufs(4), bufs(4)
    v_b, gt_b = bufs(NT), bufs(NT)
    oT_b = [bufs(8) for _ in range(8)]
    QKT_d, Vp_d, oacc_d = S["QKT_d"], S["Vp_d"], S["oacc_d"]
    QKT_b = [[bufs(8) for _ in range(8)] for _ in range(3)]
    Vp_b = [bufs(NT) for _ in range(3)]
    oacc_b = [bufs(NT) for _ in range(3)]
    idf, idb = c["idf"], c["idb"]
    with Scope(kb) as lsc:
        hsc = sbt(lsc, "cd_hsc", [128, 4, 2, 64], F32)
        with Scope(kb) as osc:
            hnT = osc.sb("cd_hnT", [128, 8, T], BF16)
            hnT_b = bufs(NT)
            with Scope(kb) as sc:
                psr_t = PsumRing(sc, "cd_pt", 2, (128, 1024), BF16)
                norm_transpose(kb, sc, c, h_src, P["norm_mix_g"][layer], hnT, hnT_b, psr_t, "cdn")
            with Scope(kb) as sc:
                psr = PsumRing(sc, "cdh_pp", 8)
                wts = [sbt(sc, "cdh_w%d" % j, [128, 8, 128], BF16) for j in range(2)]
                wi = [0]

                def next_w(c0):
                    w = wts[wi[0] % 2]
                    wi[0] += 1
                    load_w_cols(kb, w[:], w, w_in, c0, 128)
                    return w

                lbt = sbt(sc, "cdh_lb", [128, 4, 2], F32)
                kb.dma("sp", lbt[:], P["hg_lbT"].rearrange("(c p) j -> p c j", p=128), W=[lbt])
                lb2 = sbt(sc, "cdh_lb2", [128, 4, 2], F32)
                if i == 0:
                    kb.op("pool", lambda e: e.memset(lb2[:, :, 0], 1.0), W=[lb2])
                    kb.op("pool", lambda e: e.memset(lb2[:, :, 1], 0.0), W=[lb2])
                else:
                    TT(kb, "dve", lb2[:, :, 1], lbt[:, :, 1], lbt[:, :, 0], ALU.subtract, R=[lbt], W=[lb2])
                    ACTF(kb, lb2[:, :, 1], lb2[:, :, 1], AF.Sigmoid, R=[lb2], W=[lb2])
                    TS(kb, "dve", lb2[:, :, 0], lb2[:, :, 1], -1.0, ALU.mult, R=[lb2], W=[lb2], s2=1.0, op1=ALU.add)
                segm = sbt(sc, "cdh_segm", [128, T], F32)
                kb.op("pool", lambda e: e.memset(segm[:], 1.0), W=[segm])
                kb.op("pool", lambda e: e.memset(segm[:].rearrange("p (n c) -> p n c", c=64)[:, :, 0:1], 0.0), R=[segm], W=[segm])
                A = sbt(sc, "cdh_A", [128, T], F32)
                B = sbt(sc, "cdh_B", [128, T], F32)
                C = sbt(sc, "cdh_C", [128, T], F32)
                orow = [sbt(sc, "cdh_or%d" % j, [128, T], BF16) for j in range(2)]
                tq = [sbt(sc, "cdh_tq%d" % j, [128, 512], F32) for j in range(2)]
                v3 = lambda t: t[:].rearrange("p (n c) -> p n c", c=64)
                for hd in range(4):
                    w = next_w(512 + hd * 128)

                    def epi_f(tt, pt, pb):
                        ACTF(kb, A[:, tt * 512:(tt + 1) * 512], pt[:], AF.Sigmoid, R=[pb], W=[A])
                        ACTF(kb, B[:, tt * 512:(tt + 1) * 512], pt[:], AF.Sigmoid, R=[pb], W=[B], scale=-1.0)
                    proj_fm_chunk(kb, hnT, hnT_b, w, psr, epi_f)
                    TS(kb, "dve", A[:], A[:], lb2[:, hd, 0:1], ALU.mult, R=[A, lb2], W=[A], s2=lb2[:, hd, 1:2], op1=ALU.add)
                    ACTF(kb, A[:], A[:], AF.Ln, R=[A], W=[A])
                    TS(kb, "dve", B[:], B[:], lb2[:, hd, 0:1], ALU.mult, R=[B, lb2], W=[B])
                    kb.op("dve", lambda e: e.tensor_tensor_scan(out=C[:], data0=segm[:], data1=A[:], initial=0.0,
                                                                op0=ALU.mult, op1=ALU.add), R=[segm, A], W=[C])
                    TT(kb, "dve", v3(A), v3(C), v3(C)[:, :, 31:32].to_broadcast([128, 64, 64]), ALU.subtract, R=[C], W=[A])
                    ACTF(kb, hsc[:, hd, 0, :], v3(C)[:, :, 31], AF.Exp, R=[C], W=[hsc])
                    TT(kb, "dve", hsc[:, hd, 1, :], v3(C)[:, :, 63], v3(C)[:, :, 31], ALU.subtract, R=[C], W=[hsc])
                    ACTF(kb, hsc[:, hd, 1, :], hsc[:, hd, 1, :], AF.Exp, R=[hsc], W=[hsc])
                    TS(kb, "dve", C[:], A[:], 40.0, ALU.min, R=[A], W=[C])
                    ACTF(kb, C[:], C[:], AF.Exp, R=[C], W=[C])
                    w = next_w(hd * 128)
                    oq = orow[0]

                    def epi_q(tt, pt, pb):
                        t = tq[tt % 2]
                        ACTF(kb, t[:], pt[:], AF.Silu, R=[pb], W=[t])
                        TT(kb, "dve", oq[:, tt * 512:(tt + 1) * 512], t[:], C[:, tt * 512:(tt + 1) * 512], ALU.mult, R=[t, C], W=[oq])
                    proj_fm_chunk(kb, hnT, hnT_b, w, psr, epi_q)
                    kb.dma("sp", qeT_d[hd, :, :], oq[:], R=[oq], W=[qe_b[hd]])
                    TS(kb, "dve", C[:], A[:], -1.0, ALU.mult, R=[A], W=[C], s2=40.0, op1=ALU.min)
                    ACTF(kb, C[:], C[:], AF.Exp, R=[C], W=[C])
                    ok = orow[1]
                    TT(kb, "dve", ok[:], B[:], C[:], ALU.mult, R=[B, C], W=[ok])
                    kb.dma("sp", keT_d[hd, :, :], ok[:], R=[ok], W=[ke_b[hd]])
                wz = sbt(sc, "cdh_wz", [128, 8, 1024], BF16)
                load_w_cols(kb, wz[:, :, 0:512], wz, w_in, 1024, 512)
                load_w_cols(kb, wz[:, :, 512:1024], wz, w_in, 1536, 512)
                zts = [sbt(sc, "cdh_zt%d" % j, [128, 1024], BF16) for j in range(2)]
                for it in range(NT):
                    zt = zts[it % 2]
                    for half in range(2):
                        pt, pb = psr.next()
                        for dc in range(8):
                            MM(kb, pt[:], hnT[:, dc, it * 128:(it + 1) * 128], wz[:, dc, half * 512:(half + 1) * 512],
                               R=[hnT_b[it], wz], W=[pb], start=(dc == 0), stop=(dc == 7))
                        ACTF(kb, zt[:, half * 512:(half + 1) * 512], pt[:], AF.Copy if half == 0 else AF.Sigmoid, R=[pb], W=[zt])
                    kb.dma("sp", v_d[it * 128:(it + 1) * 128, :], zt[:, 0:512], R=[zt], W=[v_b[it]])
                    kb.dma("sp", gt_d[it * 128:(it + 1) * 128, :], zt[:, 512:1024], R=[zt], W=[gt_b[it]])
            with Scope(kb) as sc:
                psr = PsumRing(sc, "cda_pp", 8)
                psw = sbt(sc, "cda_psw", [128, 128], BF16)
                psw2 = sbt(sc, "cda_psw2", [128, 128], BF16)
                kb.op("pool", lambda e: e.memset(psw[:], 1.0), W=[psw])
                kb.op("pool", lambda e: e.affine_select(out=psw[:], in_=psw[:], pattern=[[-1, 128]], compare_op=ALU.is_equal,
                                                        fill=0.0, base=-64, channel_multiplier=1), R=[psw], W=[psw])
                kb.op("pool", lambda e: e.memset(psw2[:], 1.0), W=[psw2])
                kb.op("pool", lambda e: e.affine_select(out=psw2[:], in_=psw2[:], pattern=[[-1, 128]], compare_op=ALU.is_equal,
                                                        fill=0.0, base=64, channel_multiplier=1), R=[psw2], W=[psw2])
                TT(kb, "pool", psw[:], psw[:], psw2[:], ALU.add, R=[psw, psw2], W=[psw])
                wqk = sbt(sc, "cda_wqk", [128, 8, 1024], BF16)
                wv = sbt(sc, "cda_wv", [128, 8, 512], BF16)
                cs_t = [sbt(sc, "cda_cs%d" % j, [128, 2, 512], F32) for j in range(2)]
                xb = [sbt(sc, "cda_xb%d" % j, [128, 512], BF16) for j in range(2)]
                t1 = [sbt(sc, "cda_t1%d" % j, [128, 512], F32) for j in range(2)]
                t2 = [sbt(sc, "cda_t2%d" % j, [128, 512], F32) for j in range(2)]
                ob = [sbt(sc, "cda_ob%d" % j, [128, 512], BF16) for j in range(3)]
                vb = [sbt(sc, "cda_vb%d" % j, [128, 512], BF16) for j in range(2)]
                k = 0
                for p, (win, d) in enumerate(PATS):
                    load_w_cols(kb, wqk[:, :, 0:512], wqk, w_in, 2048 + p * 512, 512)
                    load_w_cols(kb, wqk[:, :, 512:1024], wqk, w_in, 3584 + p * 512, 512)
                    load_w_cols(kb, wv[:], wv, w_in, 5120 + p * 512, 512)
                    for tt in range(8):
                        cst = cs_t[tt % 2]
                        kb.dma("sp", cst[:, 0, :], P["rope_cos"][p, :, tt * 512:(tt + 1) * 512], W=[cst])
                        kb.dma("sp", cst[:, 1, :], P["rope_sin"][p, :, tt * 512:(tt + 1) * 512], W=[cst])
                        hb = hnT_b
                        for ch in range(8):
                            pt, pb = psr.next()
                            for dc in range(8):
                                MM(kb, pt[:], wqk[:, dc, ch * 128:(ch + 1) * 128], perm_cols(hnT[:, dc, :], d, tt * 512, 512),
                                   R=[wqk] + hb, W=[pb], start=(dc == 0), stop=(dc == 7))
                            x_ = xb[k % 2]
                            a1, a2, o_ = t1[k % 2], t2[k % 2], ob[k % 3]
                            k += 1
                            ACTF(kb, x_[:], pt[:], AF.Copy, R=[pb], W=[x_])
                            p2, p2b = psr.next()
                            MM(kb, p2[:], psw[:], x_[:], R=[psw, x_], W=[p2b])
                            TT(kb, "pool", a1[:], x_[:], cst[:, 0, :], ALU.mult, R=[x_, cst], W=[a1])
                            TT(kb, "dve", a2[:], p2[:], cst[:, 1, :], ALU.mult, R=[p2b, cst], W=[a2])
                            TT(kb, "dve", o_[:], a1[:], a2[:], ALU.add, R=[a1, a2], W=[o_])
                            kb.dma("sp", QKT_d[p, ch, :, tt * 512:(tt + 1) * 512], o_[:], R=[o_], W=[QKT_b[p][ch][tt]])
                    for n in range(NT):
                        pt, pb = psr.next()
                        for dc in range(8):
                            MM(kb, pt[:], perm_cols(hnT[:, dc, :], d, n * 128, 128), wv[:, dc, :], R=[wv] + hnT_b, W=[pb],
                               start=(dc == 0), stop=(dc == 7))
                        v_ = vb[n % 2]
                        ACTF(kb, v_[:], pt[:], AF.Copy, R=[pb], W=[v_])
                        kb.dma("sp", Vp_d[p, n * 128:(n + 1) * 128, :], v_[:], R=[v_], W=[Vp_b[p][n]])
        with Scope(kb) as sc:
            psr = PsumRing(sc, "hg_p", 6)
            psb = PsumRing(sc, "hg_pb", 2, (128, 1024), BF16)
            BD = sbt(sc, "hg_BD", [128, 4, 128], F32)
            kb.op("pool", lambda e: e.memset(BD[:], 1.0), W=[BD])
            affsel(kb, BD[:], [[0, 4], [1, 128]], -1, ALU.is_ge, 0.0, [BD])
            kb.op("pool", lambda e: e.memset(BD[0:64, :, 64:128], 0.0), R=[BD], W=[BD])
            ng = sbt(sc, "hg_ng", [128, 128], F32)
            kb.dma("sp", ng[:], bc_row(P["hg_norm_g"][i], 128), W=[ng])
            fac = sbt(sc, "hg_fac", [128, 4, 64], F32)
            kb.op("pool", lambda e: e.memset(fac[:], 1.0), W=[fac])
            TT(kb, "dve", fac[:, :, 0:63], hsc[:, :, 1, 0:63], hsc[:, :, 0, 1:64], ALU.mult, R=[hsc, fac], W=[fac])

            def f4(name, dt=F32):
                return sbt(sc, "hg_" + name, [128, 4, 128], dt)

            qes = [f4("qe%d" % j, BF16) for j in range(2)]
            kes = [f4("ke%d" % j, BF16) for j in range(2)]
            vts = [f4("v%d" % j, BF16) for j in range(2)]
            gts = [f4("g%d" % j, BF16) for j in range(2)]
            qeAB = sbt(sc, "hg_qeAB", [128, 4, 2, 128], BF16)
            keAB = sbt(sc, "hg_keAB", [128, 4, 2, 128], BF16)
            kb.op("pool", lambda e: e.memset(qeAB[:], 0.0), W=[qeAB])
            kb.op("pool", lambda e: e.memset(keAB[:], 0.0), W=[keAB])
            attT = f4("attT", BF16)
            St, tmp, o, on, zg, junk = f4("S"), f4("tmp"), f4("o"), f4("on"), f4("zg"), f4("junk")
            Sbf, of = f4("Sbf", BF16), f4("of", BF16)
            kb.op("pool", lambda e: e.memset(St[:], 0.0), W=[St])
            kb.op("pool", lambda e: e.memset(Sbf[:], 0.0), W=[Sbf])
            ss = sbt(sc, "hg_ss", [128, 12], F32)
            ocT = sbt(sc, "hg_ocT", [128, 4, T], BF16)
            for n in range(NT):
                cs = slice(n * 128, (n + 1) * 128)
                qe, ke, vt, gt = qes[n % 2], kes[n % 2], vts[n % 2], gts[n % 2]
                kb.dma("sp", qe[:], qeT_d[:, :, cs].rearrange("h p n -> p h n"), R=qe_b, W=[qe])
                kb.dma("sp", ke[:], keT_d[:, :, cs].rearrange("h p n -> p h n"), R=ke_b, W=[ke])
                kb.dma("sp", vt[:], h4(v_d[cs, :]), R=[v_b[n]], W=[vt])
                kb.dma("sp", gt[:], h4(gt_d[cs, :]), R=[gt_b[n]], W=[gt])
                kb.op("pool", lambda e: e.tensor_copy(out=qeAB[:, :, 0, 0:64], in_=qe[:, :, 0:64]), R=[qe], W=[qeAB])
                kb.op("pool", lambda e: e.tensor_copy(out=qeAB[:, :, 1, 64:128], in_=qe[:, :, 64:128]), R=[qe], W=[qeAB])
                pk, pkb = psb.next()
                for hd in range(4):
                    TR(kb, pk[:, hd * 128:(hd + 1) * 128], ke[:, hd, :], idb[:], R=[ke, c["b_id"]], W=[pkb])
                ACTF(kb, keAB[0:64, :, 0, :], h4(pk[0:64, 0:512]), AF.Copy, R=[pkb], W=[keAB])
                ACTF(kb, keAB[64:128, :, 1, :], h4(pk[64:128, 0:512]), AF.Copy, R=[pkb], W=[keAB])
                pA, pAb = psr.next()
                for hd in range(4):
                    MM(kb, pA[:, hd * 128:(hd + 1) * 128], ke[:, hd, :], qe[:, hd, :], R=[ke, qe], W=[pAb])
                TT(kb, "dve", attT[:], h4(pA[:, :]), BD[:], ALU.mult, R=[pAb, BD], W=[attT])
                pOs = []
                for half in range(2):
                    ck = 2 * n + half
                    pO, pOb = psr.next()
                    pOs.append((pO, pOb))
                    for hd in range(4):
                        if half == 0:
                            MM(kb, pO[:, hd * 128:(hd + 1) * 128], attT[:, hd, :], vt[:, hd, :], R=[attT, vt], W=[pOb],
                               start=True, stop=False)
                        MM(kb, pO[:, hd * 128:(hd + 1) * 128], qeAB[:, hd, half, :], Sbf[:, hd, :], R=[qeAB, Sbf], W=[pOb],
                           start=(half == 1), stop=True)
                    pK, pKb = psr.next()
                    for hd in range(4):
                        MM(kb, pK[:, hd * 128:(hd + 1) * 128], keAB[:, hd, half, :], vt[:, hd, :], R=[keAB, vt], W=[pKb])
                    fb = fac[:, :, ck:ck + 1].to_broadcast([128, 4, 128])
                    TT(kb, "dve", St[:], St[:], fb, ALU.mult, R=[St, fac], W=[St])
                    TT(kb, "dve", tmp[:], h4(pK[:, :]), fb, ALU.mult, R=[pKb, fac], W=[tmp])
                    TT(kb, "dve", St[:], St[:], tmp[:], ALU.add, R=[St, tmp], W=[St])
                    ACTF(kb, Sbf[:], St[:], AF.Copy, R=[St], W=[Sbf])
                ACTF(kb, o[:], h4(pOs[0][0][:, :]), AF.Copy, R=[pOs[0][1]], W=[o])
                TT(kb, "dve", o[:], o[:], h4(pOs[1][0][:, :]), ALU.add, R=[o, pOs[1][1]], W=[o])
                for hd in range(4):
                    ACTF(kb, junk[:, hd, :], o[:, hd, :], AF.Square, R=[o], W=[junk, ss], accum_out=ss[:, hd:hd + 1])
                ACTF(kb, ss[:, 4:8], ss[:, 0:4], AF.Sqrt, R=[ss], W=[ss], bias=_EPS_AP["ap"], scale=1.0 / 128)
                kb.op("dve", lambda e: e.reciprocal(out=ss[:, 8:12], in_=ss[:, 4:8]), R=[ss], W=[ss])
                TT(kb, "dve", on[:], o[:], bc4(ss[:, 8:12]), ALU.mult, R=[o, ss], W=[on])
                TT(kb, "pool", zg[:], gt[:], ng[:].unsqueeze(1).to_broadcast([128, 4, 128]), ALU.mult, R=[gt, ng], W=[zg])
                TT(kb, "pool", of[:], on[:], zg[:], ALU.mult, R=[on, zg], W=[of])
                pF, pFb = psb.next()
                for hd in range(4):
                    TR(kb, pF[:, hd * 128:(hd + 1) * 128], of[:, hd, :], idb[:], R=[of, c["b_id"]], W=[pFb])
                ACTF(kb, ocT[:, :, cs], h4(pF[:, 0:512]), AF.Copy, R=[pFb], W=[ocT])
            for hd in range(4):
                kb.dma("sp", oT_d[hd, :, :], ocT[:, hd, :], R=[ocT], W=oT_b[hd])
        with Scope(kb) as sc:
            psr = PsumRing(sc, "at_p", 4)
            pso = PsumRing(sc, "at_po", 4)
            mkf = sbt(sc, "at_mkf", [128, 256], F32)
            mkb = sbt(sc, "at_mkb", [128, 256], BF16)
            kb.op("pool", lambda e: e.memset(mkf[:], 0.0), W=[mkf])
            affsel(kb, mkf[:, 0:128], [[-1, 128]], 1, ALU.is_ge, -30000.0, [mkf])
            affsel(kb, mkf[:, 128:256], [[1, 128]], -1, ALU.is_ge, -30000.0, [mkf])
            kb.op("pool", lambda e: e.tensor_copy(out=mkb[:], in_=mkf[:]), R=[mkf], W=[mkb])
            QT = sbt(sc, "at_QT", [128, 4, T], BF16)
            KT = sbt(sc, "at_KT", [128, 4, T], BF16)
            Va = sbt(sc, "at_Va", [128, NT, 4, 129], BF16)
            kb.op("pool", lambda e: e.memset(Va[:, :, :, 128:129], 1.0), W=[Va])
            pTs = [sbt(sc, "at_pT%d" % j, [128, 256], BF16) for j in range(3)]
            ost = [sbt(sc, "at_os%d" % j, [128, 4, 129], F32) for j in range(2)]
            sc_ = float(128 ** -0.5)
            k = 0
            for p, (win, d) in enumerate(PATS):
                nb = (T // d) // 128
                for hd in range(4):
                    kb.dma("sp", QT[:, hd, :], QKT_d[p, hd, :, :], R=QKT_b[p][hd], W=[QT])
                    kb.dma("sp", KT[:, hd, :], QKT_d[p, 4 + hd, :, :], R=QKT_b[p][4 + hd], W=[KT])
                    kb.dma("sp", Va[:, :, hd, 0:128], Vp_d[p, :, hd * 128:(hd + 1) * 128].rearrange("(n p) c -> p n c", p=128),
                           R=Vp_b[p], W=[Va])
                for n in range(NT):
                    first = (n % nb == 0)
                    os_ = ost[n % 2]
                    for hd in range(4):
                        ps, psb_ = psr.next()
                        q_ = QT[:, hd, n * 128:(n + 1) * 128]
                        if not first:
                            MM(kb, ps[:, 0:128], KT[:, hd, (n - 1) * 128:n * 128], q_, R=[KT, QT], W=[psb_], start=True, stop=False)
                            MM(kb, ps[:, 0:128], idb[:], mkb[:, 0:128], R=[mkb, c["b_id"]], W=[psb_], start=False, stop=True)
                        MM(kb, ps[:, 128:256], KT[:, hd, n * 128:(n + 1) * 128], q_, R=[KT, QT], W=[psb_], start=True, stop=False)
                        MM(kb, ps[:, 128:256], idb[:], mkb[:, 128:256], R=[mkb, c["b_id"]], W=[psb_], start=False, stop=True)
                        pT = pTs[k % 3]
                        k += 1
                        lo = 128 if first else 0
                        ACTF(kb, pT[:, lo:256], ps[:, lo:256], AF.Exp, R=[psb_], W=[pT], scale=sc_)
                        po, pob = pso.next()
                        if not first:
                            MM(kb, po[:, 0:129], pT[:, 0:128], Va[:, n - 1, hd, :], R=[pT, Va], W=[pob], start=True, stop=False)
                        MM(kb, po[:, 0:129], pT[:, 128:256], Va[:, n, hd, :], R=[pT, Va], W=[pob], start=first, stop=True)
                        kb.op("dve", lambda e: e.tensor_copy(out=os_[:, hd, :], in_=po[:, 0:129]), R=[pob], W=[os_])
                    r, jb = n // nb, n % nb
                    dst = oacc_d[p].rearrange("(j r) c -> r j c", r=d)[r, jb * 128:(jb + 1) * 128, :]
                    tok0 = r + d * jb * 128
                    tiles = sorted(set((tok0 + d * ii) // 128 for ii in (0, 127)) | set(range((tok0) // 128, (tok0 + d * 127) // 128 + 1)))
                    kb.dma("sp", dst, os_[:].rearrange("p h c -> p (h c)"), R=[os_], W=[oacc_b[p][t_] for t_ in tiles])
        with Scope(kb) as sc:
            psb = PsumRing(sc, "mg_pb", 2, (128, 1024), BF16)
            acs = [[sbt(sc, "mg_a%d_%d" % (p, j), [128, 4, 129], F32) for p in range(3)] for j in range(2)]
            rl = sbt(sc, "mg_rl", [128, 4], F32)
            od = sbt(sc, "mg_od", [128, 4, 128], BF16)
            odT = sbt(sc, "mg_odT", [128, 4, T], BF16)
            for n in range(NT):
                a = acs[n % 2]
                for p in range(3):
                    kb.dma("sp", a[p][:].rearrange("p h c -> p (h c)"), oacc_d[p][n * 128:(n + 1) * 128, :], R=[oacc_b[p][n]], W=[a[p]])
                TT(kb, "pool", a[0][:], a[0][:], a[1][:], ALU.add, R=[a[0], a[1]], W=[a[0]])
                TT(kb, "dve", a[0][:], a[0][:], a[2][:], ALU.add, R=[a[0], a[2]], W=[a[0]])
                kb.op("dve", lambda e: e.reciprocal(out=rl[:], in_=a[0][:, :, 128]), R=[a[0]], W=[rl])
                TT(kb, "dve", od[:], a[0][:, :, 0:128], bc4(rl[:]), ALU.mult, R=[a[0], rl], W=[od])
                pF, pFb = psb.next()
                for hd in range(4):
                    TR(kb, pF[:, hd * 128:(hd + 1) * 128], od[:, hd, :], idb[:], R=[od, c["b_id"]], W=[pFb])
                ACTF(kb, odT[:, :, n * 128:(n + 1) * 128], h4(pF[:, 0:512]), AF.Copy, R=[pFb], W=[odT])
            for hd in range(4):
                kb.dma("sp", oT_d[4 + hd, :, :], odT[:, hd, :], R=[odT], W=oT_b[4 + hd])
        with Scope(kb) as sc:
            wo = sbt(sc, "cd_wo", [128, 8, D], BF16)
            for n0 in range(0, 8, 2):
                kb.dma("pool", wo[:, n0:n0 + 2, :], P["cd_w_out"][i][n0 * 128:(n0 + 2) * 128, :].rearrange("(c p) n -> p c n", p=128), W=[wo])
            proj_tm_residual(kb, sc, oT_d, oT_b, 8, wo.t, wo.b, h_src, h_dst, "cdo")


def build(n_layers=DEPTH, do_mixer=True, do_ffn=True, dbg_h=False):
    _HB.clear()
    nc = bass.Bass("TRN2", target_bir_lowering=False)

    def din(name, shape, dt=F32):
        return nc.dram_tensor(name, list(shape), dt, kind="ExternalInput").ap()

    def dscr(name, shape, dt):
        return nc.dram_tensor(name, list(shape), dt, kind="Internal").ap()

    P = {}
    x = din("x", [T, D])
    for name, shape in (("norm_mix_g", [DEPTH, D]), ("norm_ffn_g", [DEPTH, D]), ("norm_final_g", [D]),
                        ("ffn_w_gate", [DEPTH, D, FH]), ("ffn_w_up", [DEPTH, D, FH]), ("ffn_w_down", [DEPTH, FH, D]),
                        ("ab_w_in", [2, D, AB_IN]), ("dn_conv_wT", [2, 1536, 4]), ("dn_a_log", [2, 4]), ("dn_dt_bias", [2, 4]),
                        ("dn_norm_g", [2, 128]), ("sc_conv_wT", [2, 512, 3]), ("ab_w_out", [2, D, D]),
                        ("cd_w_in", [2, D, CD_IN]), ("hg_lbT", [512, 2]), ("hg_norm_g", [2, 128]), ("cd_w_out", [2, D, D]),
                        ("rope_cos", [3, 128, T]), ("rope_sin", [3, 128, T])):
        P[name] = din(name, shape)
    out = nc.dram_tensor("out", [T, D], F32, kind="ExternalOutput").ap()
    h = dscr("h_res", [T, D], F32)
    dbg_oT = nc.dram_tensor("dbg_oT", [8, 128, T], BF16, kind="ExternalOutput").ap() if dbg_h else None
    S = {"actT_d": dscr("actT_d", [NFC, 128, T], BF16),
         "qT_d": dscr("qT_d", [4, 128, T], BF16), "kT_d": dscr("kT_d", [4, 128, T], BF16), "vT_d": dscr("vT_d", [4, 128, T], BF16),
         "z_d": dscr("z_d", [T, 512], BF16), "oT_d": dscr("oT_d", [8, 128, T], BF16),
         "v_d": dscr("v_d", [T, 512], BF16), "QKT_d": dscr("QKT_d", [3, 8, 128, T], BF16),
         "Vp_d": dscr("Vp_d", [3, T, 512], BF16), "oacc_d": dscr("oacc_d", [3, T, 516], F32)}

    with ExitStack() as es:
        kb = KB(nc, es)
        with Scope(kb) as gsc:
            c = make_consts(kb, gsc)
            make_eps(kb, gsc)
            cur = x
            for layer in range(n_layers):
                if do_mixer:
                    if layer % 2 == 0:
                        ab_layer(kb, c, P, layer // 2, layer, cur, h, S)
                    else:
                        cd_layer(kb, c, P, layer // 2, layer, cur, h, S)
                    cur = h
                if do_ffn:
                    ffn_layer(kb, c, cur, h, P["norm_ffn_g"][layer], P["ffn_w_gate"][layer], P["ffn_w_up"][layer],
                              P["ffn_w_down"][layer], S["actT_d"], "f%d" % layer)
                    cur = h
            if dbg_h:
                with Scope(kb) as sc:
                    ht = sbt(sc, "dbg_h", [128, D], F32)
                    for i in range(NT):
                        kb.dma("sp", ht[:], cur[i * 128:(i + 1) * 128, :], R=[HB(cur, i)], W=[ht])
                        kb.dma("sp", out[i * 128:(i + 1) * 128, :], ht[:], R=[ht], W=[HB(out, i)])
                    ob = sbt(sc, "dbg_o", [128, T], BF16)
                    for r in range(8):
                        kb.dma("sp", ob[:], S["oT_d"][r, :, :], W=[ob])
                        kb.dma("sp", dbg_oT[r, :, :], ob[:], R=[ob], W=[ob])
            else:
                final_norm(kb, cur, P["norm_final_g"], out)
            kb.finish()
        print("instructions:", kb.n_inst, flush=True)
    return nc


def prep_inputs(inputs):
    f = lambda k: np.ascontiguousarray(np.asarray(inputs[k], dtype=np.float32))
    shared = {k: f(k) for k in ("norm_mix_g", "norm_ffn_g", "norm_final_g", "ffn_w_gate", "ffn_w_up", "ffn_w_down",
                                "ab_w_in", "dn_a_log", "dn_dt_bias", "dn_norm_g", "ab_w_out")}
    shared["dn_conv_wT"] = np.ascontiguousarray(f("dn_conv_w").transpose(0, 2, 1))
    shared["sc_conv_wT"] = np.ascontiguousarray(f("sc_conv_w").transpose(0, 2, 1))
    for k in ("cd_w_in", "hg_norm_g", "cd_w_out"):
        shared[k] = f(k)
    shared["hg_lbT"] = np.ascontiguousarray(f("hg_lower_bounds").T)
    half = 64
    inv_freq = (10000.0 ** (-np.arange(half, dtype=np.float32) * 2.0 / 128)).astype(np.float32)
    ang = np.arange(T, dtype=np.float32)[:, None] * inv_freq[None, :]
    cosF = np.concatenate([np.cos(ang), np.cos(ang)], axis=1).T.astype(np.float32)
    sinS = np.concatenate([-np.sin(ang), np.sin(ang)], axis=1).T.astype(np.float32)
    rc, rs = [], []
    for win, d in PATS:
        L = T // d
        u = np.arange(T)
        tok = (u % L) * d + (u // L)
        rc.append(cosF[:, tok])
        rs.append(sinS[:, tok])
    shared["rope_cos"] = np.ascontiguousarray(np.stack(rc))
    shared["rope_sin"] = np.ascontiguousarray(np.stack(rs))
    return shared


def kernel(**inputs):
    nc = build()
    shared = prep_inputs(inputs)
    x = np.asarray(inputs["x"], dtype=np.float32)
    in_maps = []
    for ci in range(8):
        m = dict(shared)
        m["x"] = np.ascontiguousarray(x[ci])
        in_maps.append(m)
    res = run_bass_kernel_spmd(nc, in_maps, core_ids=list(range(8)))
    return np.stack([np.asarray(r["out"], dtype=np.float32) for r in res.results], axis=0)
```

```python
import numpy as np
from contextlib import ExitStack
import concourse.bass as bass
import concourse.mybir as mybir
from concourse.bass_utils import run_bass_kernel_spmd

F32 = mybir.dt.float32
BF16 = mybir.dt.bfloat16
AF = mybir.ActivationFunctionType
ALU = mybir.AluOpType

T = 4096
D = 1024
NT = T // 128
DEPTH = 4
FH = 2816
NFC = FH // 128
EPS = 1e-6
AB_IN = 3592
CD_IN = 6656
SAME_ENGINE_SYNC = True
GDN_IL = False
ATT_PIPE = True
ROPE_PIPE = True


class Buf:
    __slots__ = ("w", "r")

    def __init__(self):
        self.w = None
        self.r = []


def bufs(n):
    return [Buf() for _ in range(n)]


class KB:
    def __init__(self, nc, es):
        self.nc = nc
        self.es = es
        self.e = {"pe": nc.tensor, "act": nc.scalar, "dve": nc.vector, "pool": nc.gpsimd, "sp": nc.sync}
        self.sems = {}
        self.cnt = {}
        for k in self.e:
            self.sems[k] = es.enter_context(nc.semaphore("s_" + k))
            self.cnt[k] = 0
        self.seen = {k: {} for k in self.e}
        self.dq = {}
        self.dqi = {}
        for q, n in (("sp", 24), ("pool", 16), ("act", 8)):
            lst = []
            for i in range(n):
                key = "d_%s%d" % (q, i)
                self.sems[key] = es.enter_context(nc.semaphore(key))
                lst.append([key, 0])
            self.dq[q] = lst
            self.dqi[q] = 0
        self.n_inst = 0
        self.uid = 0
        self.marks = []

    def _wait(self, eng, tk):
        if tk is None:
            return
        key, val = tk
        if key == eng and (eng == "pe" or not SAME_ENGINE_SYNC):
            return
        if self.seen[eng].get(key, 0) >= val:
            return
        self.seen[eng][key] = val
        self.e[eng].wait_ge(self.sems[key], val)
        self.n_inst += 1

    def _deps(self, eng, R, W):
        for b in R:
            self._wait(eng, b.w)
        for b in W:
            self._wait(eng, b.w)
            for tk in b.r:
                self._wait(eng, tk)

    def op(self, eng, fn, R=(), W=()):
        return self._op(eng, fn, [getattr(x, "b", x) for x in R], [getattr(x, "b", x) for x in W])

    def dma(self, q, out, in_, R=(), W=()):
        return self._dma(q, out, in_, [getattr(x, "b", x) for x in R], [getattr(x, "b", x) for x in W])

    def _commit(self, tk, R, W):
        for b in R:
            b.r.append(tk)
            if len(b.r) > 64:
                b.r = b.r[-48:]
        for b in W:
            b.w = tk
            b.r = []

    def _op(self, eng, fn, R=(), W=()):
        self._deps(eng, R, W)
        ins = fn(self.e[eng])
        self.cnt[eng] += 1
        ins.then_inc(self.sems[eng], 1)
        tk = (eng, self.cnt[eng])
        self._commit(tk, R, W)
        self.n_inst += 1
        return tk

    def _dma(self, q, out, in_, R=(), W=()):
        lst = self.dq[q]
        i = self.dqi[q]
        self.dqi[q] = (i + 1) % len(lst)
        key, val = lst[i]
        if val:
            self._wait(q, (key, val))
        self._deps(q, R, W)
        ins = self.e[q].dma_start(out=out, in_=in_)
        val += 16
        lst[i][1] = val
        ins.then_inc(self.sems[key], 16)
        tk = (key, val)
        self._commit(tk, R, W)
        self.n_inst += 1
        return tk

    def mark(self, name):
        self.marks.append((name, dict(self.cnt)))

    def barrier(self):
        tks = [(k, self.cnt[k]) for k in self.e if self.cnt[k]]
        for q in self.dq:
            for key, val in self.dq[q]:
                if val:
                    tks.append((key, val))
        for eng in self.e:
            for tk in tks:
                if tk[0] == eng:
                    continue
                self._wait(eng, tk)

    def finish(self):
        tks = []
        for q in self.dq:
            for key, val in self.dq[q]:
                if val:
                    tks.append((key, val))
        for tk in tks:
            self._wait("sp", tk)


class Scope:
    def __init__(self, kb, name=None):
        self.kb = kb
        self.es = ExitStack()
        self.name = name

    def __enter__(self):
        self.es.__enter__()
        return self

    def __exit__(self, *a):
        self.kb.barrier()
        if self.name:
            self.kb.mark(self.name)
        return self.es.__exit__(*a)

    def sb(self, name, shape, dt):
        self.kb.uid += 1
        return self.es.enter_context(self.kb.nc.sbuf_tensor("%s_u%d" % (name, self.kb.uid), list(shape), dt))

    def ps(self, name, shape, dt):
        self.kb.uid += 1
        return self.es.enter_context(self.kb.nc.psum_tensor("%s_u%d" % (name, self.kb.uid), list(shape), dt))


class TB:
    def __init__(self, t):
        self.t = t
        self.b = Buf()

    def __getitem__(self, idx):
        return self.t[idx]


def sbt(sc, name, shape, dt):
    return TB(sc.sb(name, shape, dt))


def TT(kb, eng, out, a, b, op, R, W):
    return kb.op(eng, lambda e: e.tensor_tensor(out=out, in0=a, in1=b, op=op), R, W)


def TS(kb, eng, out, a, s1, op0, R, W, s2=None, op1=None):
    if op1 is None:
        return kb.op(eng, lambda e: e.tensor_scalar(out=out, in0=a, scalar1=s1, scalar2=None, op0=op0), R, W)
    return kb.op(eng, lambda e: e.tensor_scalar(out=out, in0=a, scalar1=s1, scalar2=s2, op0=op0, op1=op1), R, W)


def STT(kb, eng, out, a, sc_, b, op0, op1, R, W):
    return kb.op(eng, lambda e: e.scalar_tensor_tensor(out=out, in0=a, scalar=sc_, in1=b, op0=op0, op1=op1), R, W)


def ACTF(kb, out, in_, func, R, W, **kw):
    return kb.op("act", lambda e: e.activation(out=out, in_=in_, func=func, **kw), R, W)


def MM(kb, out, lhsT, rhs, R, W, start=True, stop=True):
    return kb.op("pe", lambda e: e.matmul(out, lhsT=lhsT, rhs=rhs, start=start, stop=stop), R, W)


def TR(kb, out, in_, ident, R, W):
    return kb.op("pe", lambda e: e.transpose(out=out, in_=in_, identity=ident), R, W)


def h4(ap, n=128):
    return ap.rearrange("p (h n) -> p h n", h=4)


def bc4(ap, n=128):
    return ap.unsqueeze(2).to_broadcast([128, 4, n])


class PsumRing:
    def __init__(self, sc, name, n, shape=(128, 512), dt=F32):
        self.t = [sc.ps("%s%d" % (name, i), shape, dt) for i in range(n)]
        self.b = bufs(n)
        self.i = 0

    def next(self):
        i = self.i
        self.i = (i + 1) % len(self.t)
        return self.t[i], self.b[i]


def bc_row(ap1d, n):
    return ap1d.partition_broadcast(n)


def make_consts(kb, sc):
    c = {}
    c["idb"] = sc.sb("idb", [128, 128], BF16)
    c["idf"] = sc.sb("idf", [128, 128], F32)
    c["onesf"] = sc.sb("onesf", [128, 128], F32)
    c["b_id"] = Buf()
    kb.op("pool", lambda e: e.memset(c["onesf"][:], 1.0), W=[c["b_id"]])
    kb.op("pool", lambda e: e.memset(c["idb"][:], 1.0), W=[c["b_id"]])
    kb.op("pool", lambda e: e.affine_select(out=c["idb"][:], in_=c["idb"][:], pattern=[[-1, 128]],
                                            compare_op=ALU.is_equal, fill=0.0, base=0, channel_multiplier=1),
          R=[c["b_id"]], W=[c["b_id"]])
    kb.op("pool", lambda e: e.memset(c["idf"][:], 1.0), W=[c["b_id"]])
    kb.op("pool", lambda e: e.affine_select(out=c["idf"][:], in_=c["idf"][:], pattern=[[-1, 128]],
                                            compare_op=ALU.is_equal, fill=0.0, base=0, channel_multiplier=1),
          R=[c["b_id"]], W=[c["b_id"]])
    return c


def norm_transpose(kb, sc, c, h_src, g_row, hnT, hnT_b, psr, tag):
    gb = sc.sb(tag + "_g", [128, D], F32)
    gb_b = Buf()
    kb.dma("sp", gb[:], bc_row(g_row, 128), W=[gb_b])
    ht = [sc.sb(tag + "_h%d" % i, [128, D], F32) for i in range(3)]
    ht_b = bufs(3)
    junk = sc.sb(tag + "_junk", [128, D], BF16)
    junk_b = Buf()
    hn = [sc.sb(tag + "_hn%d" % i, [128, D], BF16) for i in range(3)]
    hn_b = bufs(3)
    st = sc.sb(tag + "_st", [128, NT, 4], F32)
    st_b = bufs(NT)
    for i in range(NT):
        s = i % 3
        kb.dma("sp", ht[s][:], h_src[i * 128:(i + 1) * 128, :], R=[HB(h_src, i)], W=[ht_b[s]])
        kb.op("act", lambda e: e.activation(out=junk[:], in_=ht[s][:], func=AF.Square, scale=float(D ** -0.5),
                                            accum_out=st[:, i, 0:1]), R=[ht_b[s]], W=[junk_b, st_b[i]])
        kb.op("act", lambda e: e.activation(out=st[:, i, 1:2], in_=st[:, i, 0:1], func=AF.Sqrt, bias=EPS_AP(kb), scale=1.0),
              R=[st_b[i]], W=[st_b[i]])
        kb.op("dve", lambda e: e.reciprocal(out=st[:, i, 2:3], in_=st[:, i, 1:2]), R=[st_b[i]], W=[st_b[i]])
        kb.op("dve", lambda e: e.scalar_tensor_tensor(out=hn[s][:], in0=ht[s][:], scalar=st[:, i, 2:3], in1=gb[:],
                                                      op0=ALU.mult, op1=ALU.mult),
              R=[ht_b[s], st_b[i], gb_b], W=[hn_b[s]])
        pt, pb = psr.next()
        for dc in range(8):
            kb.op("pe", lambda e: e.transpose(out=pt[:, dc * 128:(dc + 1) * 128], in_=hn[s][:, dc * 128:(dc + 1) * 128],
                                              identity=c["idb"][:]), R=[hn_b[s], c["b_id"]], W=[pb])
        kb.op("act", lambda e: e.copy(out=hnT[:, :, i * 128:(i + 1) * 128],
                                      in_=pt[:, :].rearrange("p (c n) -> p c n", c=8)), R=[pb], W=[hnT_b[i]])


_EPS_AP = {}


def EPS_AP(kb):
    return _EPS_AP["ap"]


def make_eps(kb, sc):
    t = sc.sb("eps_t", [128, 1], F32)
    b = Buf()
    kb.op("pool", lambda e: e.memset(t[:], EPS), W=[b])
    kb.barrier()
    _EPS_AP["ap"] = t[:, 0:1]


def load_w_cols(kb, wdst, wdst_b, w_dram, c0, cw, q="pool"):
    src = w_dram[:, c0:c0 + cw].rearrange("(c p) n -> p c n", p=128)
    kb.dma(q, wdst, src, W=[wdst_b])


def ffn_layer(kb, c, h_src, h_dst, g_row, wg, wu, wd, actT_d, lname):
    actT_b = [bufs(8) for _ in range(NFC)]
    with Scope(kb, "ffnA") as sc:
        hnT = sc.sb(lname + "hnT", [128, 8, T], BF16)
        hnT_b = bufs(NT)
        psr_t = PsumRing(sc, lname + "pt", 2, (128, 1024), BF16)
        norm_transpose(kb, sc, c, h_src, g_row, hnT, hnT_b, psr_t, lname + "n")
        kb.mark("ffn_norm")
        psr = PsumRing(sc, lname + "pp", 6)
        wgt = [sc.sb(lname + "wg%d" % i, [128, 8, 512], BF16) for i in range(2)]
        wut = [sc.sb(lname + "wu%d" % i, [128, 8, 512], BF16) for i in range(2)]
        wg_b = bufs(2)
        wu_b = bufs(2)
        sg = [sc.sb(lname + "sg%d" % i, [128, 512], F32) for i in range(2)]
        sg_b = bufs(2)
        arow = [sc.sb(lname + "ar%d" % i, [128, T], BF16) for i in range(2)]
        arow_b = bufs(2)
        k = 0
        for n in range(NFC):
            s = n % 2
            gi, nn = n // 4, n % 4
            ws = gi % 2
            if nn == 0:
                cw = min(512, FH - gi * 512)
                load_w_cols(kb, wgt[ws][:, :, 0:cw], wg_b[ws], wg, gi * 512, cw)
                load_w_cols(kb, wut[ws][:, :, 0:cw], wu_b[ws], wu, gi * 512, cw)
            for tt in range(8):
                pg, pgb = psr.next()
                pu, pub = psr.next()
                hb = hnT_b[tt * 4:(tt + 1) * 4]
                for dc in range(8):
                    kb.op("pe", lambda e: e.matmul(pg[:], lhsT=wgt[ws][:, dc, nn * 128:(nn + 1) * 128],
                                                   rhs=hnT[:, dc, tt * 512:(tt + 1) * 512],
                                                   start=(dc == 0), stop=(dc == 7)), R=[wg_b[ws]] + hb, W=[pgb])
                for dc in range(8):
                    kb.op("pe", lambda e: e.matmul(pu[:], lhsT=wut[ws][:, dc, nn * 128:(nn + 1) * 128],
                                                   rhs=hnT[:, dc, tt * 512:(tt + 1) * 512],
                                                   start=(dc == 0), stop=(dc == 7)), R=[wu_b[ws]] + hb, W=[pub])
                ss = k % 2
                k += 1
                kb.op("act", lambda e: e.activation(out=sg[ss][:], in_=pg[:], func=AF.Silu), R=[pgb], W=[sg_b[ss]])
                kb.op("dve", lambda e: e.tensor_tensor(out=arow[s][:, tt * 512:(tt + 1) * 512], in0=sg[ss][:], in1=pu[:],
                                                       op=ALU.mult), R=[sg_b[ss], pub], W=[arow_b[s]])
            kb.dma("sp", actT_d[n, :, :], arow[s][:], R=[arow_b[s]], W=[b for b in actT_b[n]])
    with Scope(kb, "ffnB") as sc:
        wdt = sc.sb(lname + "wd", [128, NFC, D], BF16)
        wd_b = Buf()
        for n0 in range(0, NFC, 2):
            kb.dma("pool", wdt[:, n0:n0 + 2, :], wd[n0 * 128:(n0 + 2) * 128, :].rearrange("(c p) n -> p c n", p=128), W=[wd_b])
        proj_tm_residual(kb, sc, actT_d, actT_b, NFC, wdt, wd_b, h_src, h_dst, lname + "d")


def proj_tm_residual(kb, sc, xT_d, xT_b, nk, wt, wt_b, h_src, h_dst, tag):
    psr = PsumRing(sc, tag + "ps", 4)
    xt = [sc.sb(tag + "x%d" % i, [128, nk, 512], BF16) for i in range(2)]
    xt_b = bufs(2)
    ht = [sc.sb(tag + "h%d" % i, [128, D], F32) for i in range(3)]
    ht_b = bufs(3)
    j = 0
    for tt in range(8):
        s = tt % 2
        kb.dma("sp", xt[s][:], xT_d[:, :, tt * 512:(tt + 1) * 512].rearrange("k p n -> p k n"),
               R=[xT_b[k][tt] for k in range(nk)], W=[xt_b[s]])
        for sub in range(4):
            i = tt * 4 + sub
            hs = j % 3
            j += 1
            kb.dma("sp", ht[hs][:], h_src[i * 128:(i + 1) * 128, :], R=[HB(h_src, i)], W=[ht_b[hs]])
            for half in range(2):
                pt, pb = psr.next()
                for k in range(nk):
                    kb.op("pe", lambda e: e.matmul(pt[:], lhsT=xt[s][:, k, sub * 128:(sub + 1) * 128],
                                                   rhs=wt[:, k, half * 512:(half + 1) * 512],
                                                   start=(k == 0), stop=(k == nk - 1)), R=[xt_b[s], wt_b], W=[pb])
                kb.op("dve", lambda e: e.tensor_tensor(out=ht[hs][:, half * 512:(half + 1) * 512],
                                                       in0=ht[hs][:, half * 512:(half + 1) * 512], in1=pt[:], op=ALU.add),
                      R=[pb, ht_b[hs]], W=[ht_b[hs]])
            kb.dma("sp", h_dst[i * 128:(i + 1) * 128, :], ht[hs][:], R=[ht_b[hs]], W=[HB(h_dst, i)])


_HB = {}


def HB(ap, i):
    key = ap.tensor.name
    if key not in _HB:
        _HB[key] = bufs(NT)
    return _HB[key][i]


def final_norm(kb, h_src, g_row, out):
    with Scope(kb, "final") as sc:
        gb = sc.sb("fn_g", [128, D], F32)
        gb_b = Buf()
        kb.dma("sp", gb[:], bc_row(g_row, 128), W=[gb_b])
        ht = [sc.sb("fn_h%d" % i, [128, D], F32) for i in range(3)]
        ht_b = bufs(3)
        junk = sc.sb("fn_junk", [128, D], BF16)
        junk_b = Buf()
        st = sc.sb("fn_st", [128, NT, 4], F32)
        st_b = bufs(NT)
        for i in range(NT):
            s = i % 3
            kb.dma("sp", ht[s][:], h_src[i * 128:(i + 1) * 128, :], R=[HB(h_src, i)], W=[ht_b[s]])
            kb.op("act", lambda e: e.activation(out=junk[:], in_=ht[s][:], func=AF.Square, scale=float(D ** -0.5),
                                                accum_out=st[:, i, 0:1]), R=[ht_b[s]], W=[junk_b, st_b[i]])
            kb.op("act", lambda e: e.activation(out=st[:, i, 1:2], in_=st[:, i, 0:1], func=AF.Sqrt, bias=EPS_AP(kb), scale=1.0),
                  R=[st_b[i]], W=[st_b[i]])
            kb.op("dve", lambda e: e.reciprocal(out=st[:, i, 2:3], in_=st[:, i, 1:2]), R=[st_b[i]], W=[st_b[i]])
            kb.op("dve", lambda e: e.scalar_tensor_tensor(out=ht[s][:], in0=ht[s][:], scalar=st[:, i, 2:3], in1=gb[:],
                                                          op0=ALU.mult, op1=ALU.mult),
                  R=[ht_b[s], st_b[i], gb_b], W=[ht_b[s]])
            kb.dma("sp", out[i * 128:(i + 1) * 128, :], ht[s][:], R=[ht_b[s]], W=[HB(out, i)])


def proj_fm_chunk(kb, hnT, hnT_b, wt, psr, epi):
    for tt in range(8):
        pt, pb = psr.next()
        hb = hnT_b[tt * 4:(tt + 1) * 4]
        for dc in range(8):
            MM(kb, pt[:], wt[:, dc, :], hnT[:, dc, tt * 512:(tt + 1) * 512], R=[wt] + hb, W=[pb], start=(dc == 0), stop=(dc == 7))
        epi(tt, pt, pb)


def affsel(kb, t, pattern, cm, op, fill, W):
    kb.op("pool", lambda e: e.affine_select(out=t, in_=t, pattern=pattern, compare_op=op, fill=fill, base=0,
                                            channel_multiplier=cm), R=W, W=W)


def ab_layer(kb, c, P, i, layer, h_src, h_dst, S):
    w_in = P["ab_w_in"][i]
    qkvT_d = [S["qT_d"], S["kT_d"], S["vT_d"]]
    qkvT_b = [bufs(4) for _ in range(3)]
    z_d = S["z_d"]
    z_b = bufs(NT)
    oT_d = S["oT_d"]
    oT_b = [bufs(8) for _ in range(8)]
    with Scope(kb) as lsc:
        ba = sbt(lsc, "ab_ba", [128, NT, 8], F32)
        with Scope(kb, "ab_proj") as sc:
            hnT = sc.sb("ab_hnT", [128, 8, T], BF16)
            hnT_b = bufs(NT)
            psr_t = PsumRing(sc, "ab_pt", 2, (128, 1024), BF16)
            norm_transpose(kb, sc, c, h_src, P["norm_mix_g"][layer], hnT, hnT_b, psr_t, "abn")
            kb.mark("ab_norm")
            psr = PsumRing(sc, "ab_pp", 6)
            wts = [sbt(sc, "ab_w%d" % j, [128, 8, 128], BF16) for j in range(2)]
            wi = [0]

            def next_w(c0):
                w = wts[wi[0] % 2]
                wi[0] += 1
                load_w_cols(kb, w[:], w, w_in, c0, 128)
                return w

            cw = sbt(sc, "ab_cw", [128, 12, 4], F32)
            kb.dma("sp", cw[:], P["dn_conv_wT"][i].rearrange("(c p) j -> p c j", p=128), W=[cw])
            scw = sbt(sc, "ab_scw", [128, 4, 3], F32)
            kb.dma("sp", scw[:], P["sc_conv_wT"][i].rearrange("(c p) j -> p c j", p=128), W=[scw])
            xpre = [sbt(sc, "ab_xp%d" % j, [128, 3 + T], F32) for j in range(2)]
            for xp in xpre:
                kb.op("pool", lambda e: e.memset(xp[:, 0:3], 0.0), W=[xp])
            acc = sbt(sc, "ab_acc", [128, T], F32)
            y = sbt(sc, "ab_y", [128, T], F32)
            orow = [sbt(sc, "ab_or%d" % j, [128, T], BF16) for j in range(2)]
            rn = [sbt(sc, "ab_rn%d" % j, [128, 512], F32) for j in range(2)]
            for j in range(12):
                xp = xpre[j % 2]
                w = next_w(j * 128)
                proj_fm_chunk(kb, hnT, hnT_b, w, psr,
                              lambda tt, pt, pb: ACTF(kb, xp[:, 3 + tt * 512:3 + (tt + 1) * 512], pt[:], AF.Copy, R=[pb], W=[xp]))
                TS(kb, "dve", acc[:], xp[:, 0:T], cw[:, j, 0:1], ALU.mult, R=[xp, cw], W=[acc])
                for t in range(1, 4):
                    STT(kb, "dve", acc[:], xp[:, t:t + T], cw[:, j, t:t + 1], acc[:], ALU.mult, ALU.add, R=[xp, cw, acc], W=[acc])
                ACTF(kb, y[:], acc[:], AF.Silu, R=[acc], W=[y])
                orw = orow[j % 2]
                if j < 8:
                    ACTF(kb, acc[:], y[:], AF.Square, R=[y], W=[acc])
                    qs = float(128 ** -0.5) if j < 4 else 1.0
                    for tt in range(8):
                        pt, pb = psr.next()
                        MM(kb, pt[:], c["onesf"][:], acc[:, tt * 512:(tt + 1) * 512], R=[acc, c["b_id"]], W=[pb])
                        r = rn[tt % 2]
                        ACTF(kb, r[:], pt[:], AF.Sqrt, R=[pb], W=[r], bias=_EPS_AP["ap"], scale=1.0)
                        kb.op("dve", lambda e: e.reciprocal(out=r[:], in_=r[:]), R=[r], W=[r])
                        STT(kb, "dve", orw[:, tt * 512:(tt + 1) * 512], y[:, tt * 512:(tt + 1) * 512], qs, r[:],
                            ALU.mult, ALU.mult, R=[y, r], W=[orw])
                else:
                    kb.op("dve", lambda e: e.tensor_copy(out=orw[:], in_=y[:]), R=[y], W=[orw])
                kb.dma("sp", qkvT_d[j // 4][j % 4, :, :], orw[:], R=[orw], W=[qkvT_b[j // 4][j % 4]])
            for j in range(4):
                xp = xpre[j % 2]
                orw = orow[j % 2]
                w = next_w(2568 + j * 128)
                proj_fm_chunk(kb, hnT, hnT_b, w, psr,
                              lambda tt, pt, pb: ACTF(kb, y[:, tt * 512:(tt + 1) * 512], pt[:], AF.Copy, R=[pb], W=[y]))
                w = next_w(3080 + j * 128)
                proj_fm_chunk(kb, hnT, hnT_b, w, psr,
                              lambda tt, pt, pb: TT(kb, "dve", xp[:, 3 + tt * 512:3 + (tt + 1) * 512], pt[:],
                                                    y[:, tt * 512:(tt + 1) * 512], ALU.mult, R=[pb, y], W=[xp]))
                TS(kb, "dve", acc[:], xp[:, 1:1 + T], scw[:, j, 0:1], ALU.mult, R=[xp, scw], W=[acc])
                for t in range(1, 3):
                    STT(kb, "dve", acc[:], xp[:, 1 + t:1 + t + T], scw[:, j, t:t + 1], acc[:], ALU.mult, ALU.add,
                        R=[xp, scw, acc], W=[acc])
                w = next_w(2056 + j * 128)
                proj_fm_chunk(kb, hnT, hnT_b, w, psr,
                              lambda tt, pt, pb: TT(kb, "dve", orw[:, tt * 512:(tt + 1) * 512], pt[:],
                                                    acc[:, tt * 512:(tt + 1) * 512], ALU.mult, R=[pb, acc], W=[orw]))
                kb.dma("sp", oT_d[4 + j, :, :], orw[:], R=[orw], W=oT_b[4 + j])
            wz = sbt(sc, "ab_wz", [128, 8, 512], BF16)
            load_w_cols(kb, wz[:], wz, w_in, 1536, 512)
            zts = [sbt(sc, "ab_zt%d" % j, [128, 512], BF16) for j in range(2)]
            for it in range(NT):
                pt, pb = psr.next()
                for dc in range(8):
                    MM(kb, pt[:], hnT[:, dc, it * 128:(it + 1) * 128], wz[:, dc, :], R=[hnT_b[it], wz], W=[pb],
                       start=(dc == 0), stop=(dc == 7))
                zt = zts[it % 2]
                ACTF(kb, zt[:], pt[:], AF.Silu, R=[pb], W=[zt])
                kb.dma("sp", z_d[it * 128:(it + 1) * 128, :], zt[:], R=[zt], W=[z_b[it]])
            wba = sbt(sc, "ab_wba", [128, 8, 8], BF16)
            load_w_cols(kb, wba[:], wba, w_in, 2048, 8)
            pt, pb = psr.next()
            for it in range(NT):
                for dc in range(8):
                    MM(kb, pt[:, it * 8:(it + 1) * 8], hnT[:, dc, it * 128:(it + 1) * 128], wba[:, dc, :], R=[hnT_b[it], wba],
                       W=[pb], start=(dc == 0), stop=(dc == 7))
            kb.op("dve", lambda e: e.tensor_copy(out=ba[:], in_=pt[:, 0:NT * 8].rearrange("p (n c) -> p n c", c=8)),
                  R=[pb], W=[ba])
        with Scope(kb, "gdn") as sc:
            psr = PsumRing(sc, "gd_p", 6)
            psb = PsumRing(sc, "gd_pb", 2, (128, 1024), BF16)
            tri = sbt(sc, "gd_tri", [128, 128], F32)
            kb.op("pool", lambda e: e.memset(tri[:], 1.0), W=[tri])
            affsel(kb, tri[:], [[1, 128]], -1, ALU.is_ge, 0.0, [tri])
            NM = sbt(sc, "gd_NM", [128, 4, 128], F32)
            kb.op("pool", lambda e: e.memset(NM[:], 0.0), W=[NM])
            affsel(kb, NM[:], [[0, 4], [1, 128]], -1, ALU.is_ge, -1.0e4, [NM])
            SM = sbt(sc, "gd_SM", [128, 4, 128], F32)
            kb.op("pool", lambda e: e.memset(SM[:], 1.0), W=[SM])
            affsel(kb, SM[:], [[0, 4], [1, 128]], -1, ALU.is_gt, 0.0, [SM])
            I4 = sbt(sc, "gd_I4", [128, 4, 128], F32)
            kb.op("pool", lambda e: e.memset(I4[:], 1.0), W=[I4])
            affsel(kb, I4[:], [[0, 4], [1, 128]], -1, ALU.is_equal, 0.0, [I4])
            dtb = sbt(sc, "gd_dtb", [128, 4], F32)
            kb.dma("sp", dtb[:], bc_row(P["dn_dt_bias"][i], 128), W=[dtb])
            nA = sbt(sc, "gd_nA", [128, 4], F32)
            kb.dma("sp", nA[:], bc_row(P["dn_a_log"][i], 128), W=[nA])
            ACTF(kb, nA[:], nA[:], AF.Exp, R=[nA], W=[nA])
            TS(kb, "dve", nA[:], nA[:], -1.0, ALU.mult, R=[nA], W=[nA])
            ng = sbt(sc, "gd_ng", [128, 128], F32)
            kb.dma("sp", ng[:], bc_row(P["dn_norm_g"][i], 128), W=[ng])
            beta = sbt(sc, "gd_beta", [128, NT, 4], F32)
            ACTF(kb, beta[:], ba[:, :, 0:4], AF.Sigmoid, R=[ba], W=[beta])
            nbeta = sbt(sc, "gd_nbeta", [128, NT, 4], F32)
            TS(kb, "dve", nbeta[:], beta[:], -1.0, ALU.mult, R=[beta], W=[nbeta])
            g = sbt(sc, "gd_g", [128, NT, 4], F32)
            TT(kb, "dve", g[:], ba[:, :, 4:8], dtb[:].unsqueeze(1).to_broadcast([128, NT, 4]), ALU.add, R=[ba, dtb], W=[g])
            ACTF(kb, g[:], g[:], AF.Exp, R=[g], W=[g])
            TS(kb, "dve", g[:], g[:], 1.0, ALU.add, R=[g], W=[g])
            ACTF(kb, g[:], g[:], AF.Ln, R=[g], W=[g])
            TT(kb, "dve", g[:], g[:], nA[:].unsqueeze(1).to_broadcast([128, NT, 4]), ALU.mult, R=[g, nA], W=[g])

            def f4(name, dt=F32):
                return sbt(sc, "gd_" + name, [128, 4, 128], dt)

            qkvs = [sbt(sc, "gd_qkv%d" % j, [128, 3, 4, 128], BF16) for j in range(2)]
            zts = [sbt(sc, "gd_zt%d" % j, [128, 512], BF16) for j in range(2)]
            Grep, dm, Dt, A2, P0T, P0 = f4("Grep"), f4("dm"), f4("Dt"), f4("A2"), f4("P0T"), f4("P0")
            Pb = [f4("Pa"), f4("Pb")]
            PTb = [f4("PTa"), f4("PTb")]
            XTb = [f4("XTa"), f4("XTb")]
            XTbf = f4("XTbf", BF16)
            tmp, o1, o, on, zg, junk = f4("tmp"), f4("o1"), f4("o"), f4("on"), f4("zg"), f4("junk")
            kd, vtm, qkT, Tm, u, of = f4("kd", BF16), f4("vtm", BF16), f4("qkT", BF16), f4("Tm", BF16), f4("u", BF16), f4("of", BF16)
            St = f4("S")
            Sbf = f4("Sbf", BF16)
            kb.op("pool", lambda e: e.memset(St[:], 0.0), W=[St])
            kb.op("pool", lambda e: e.memset(Sbf[:], 0.0), W=[Sbf])
            sm = sbt(sc, "gd_sm", [128, 8, 4], F32)
            ss = sbt(sc, "gd_ss", [128, 12], F32)
            oaT = sbt(sc, "gd_oaT", [128, 4, T], BF16)
            idf, idb = c["idf"], c["idb"]
            sm2 = [sm, sbt(sc, "gd_sm_b", [128, 8, 4], F32)]
            kd2 = [kd, f4("kd_b", BF16)]
            vtm2 = [vtm, f4("vtm_b", BF16)]
            qkT2 = [qkT, f4("qkT_b", BF16)]
            XTbf2 = [XTbf, f4("XTbf_b", BF16)]

            def stage1(n):
                par = n % 2
                sm, kd, vtm, qkT, XTbf = sm2[par], kd2[par], vtm2[par], qkT2[par], XTbf2[par]
                cs = slice(n * 128, (n + 1) * 128)
                qkv = qkvs[n % 2]
                zt = zts[n % 2]
                for a in range(3):
                    kb.dma("sp", qkv[:, a, :, :], qkvT_d[a][:, :, cs].rearrange("h p n -> p h n"), R=qkvT_b[a], W=[qkv])
                kb.dma("sp", zt[:], z_d[cs, :], R=[z_b[n]], W=[zt])
                qT = lambda hd: qkv[:, 0, hd, :]
                kT = lambda hd: qkv[:, 1, hd, :]
                kb.op("pool", lambda e: e.tensor_copy(out=Grep[:], in_=bc4(g[:, n, :])), R=[g], W=[Grep])
                yield
                pgr, pgrb = psr.next()
                for hd in range(4):
                    MM(kb, pgr[:, hd * 128:(hd + 1) * 128], Grep[:, hd, :], tri[:], R=[Grep, tri], W=[pgrb])
                yield
                pgc, pgcb = psr.next()
                MM(kb, pgc[:, 0:4], tri[:], g[:, n, :], R=[tri, g], W=[pgcb])
                ACTF(kb, sm[:, 0, :], pgc[:, 0:4], AF.Copy, R=[pgcb], W=[sm])
                ACTF(kb, sm[:, 1, :], pgc[:, 0:4], AF.Exp, R=[pgcb], W=[sm])
                TS(kb, "dve", sm[:, 2, :], sm[:, 1, :], -1.0, ALU.mult, R=[sm], W=[sm])
                kb.op("dve", lambda e: e.tensor_copy(out=sm[:, 3, :], in_=h4(pgr[:, :])[:, :, 127]), R=[pgrb], W=[sm])
                ACTF(kb, sm[:, 4, :], sm[:, 3, :], AF.Exp, R=[sm], W=[sm])
                TT(kb, "dve", sm[:, 5, :], sm[:, 3, :], sm[:, 0, :], ALU.subtract, R=[sm], W=[sm])
                ACTF(kb, sm[:, 5, :], sm[:, 5, :], AF.Exp, R=[sm], W=[sm])
                for hd in range(4):
                    STT(kb, "dve", dm[:, hd, :], pgr[:, hd * 128:(hd + 1) * 128], sm[:, 0, hd:hd + 1], NM[:, hd, :],
                        ALU.subtract, ALU.add, R=[pgrb, sm, NM], W=[dm])
                ACTF(kb, Dt[:], dm[:], AF.Exp, R=[dm], W=[Dt])
                yield
                pkv, pkvb = psb.next()
                for hd in range(4):
                    TR(kb, pkv[:, hd * 128:(hd + 1) * 128], qkv[:, 1, hd, :], idb[:], R=[qkv, c["b_id"]], W=[pkvb])
                for hd in range(4):
                    TR(kb, pkv[:, 512 + hd * 128:512 + (hd + 1) * 128], qkv[:, 2, hd, :], idb[:], R=[qkv, c["b_id"]], W=[pkvb])
                TT(kb, "dve", kd[:], h4(pkv[:, 0:512]), bc4(sm[:, 5, :]), ALU.mult, R=[pkvb, sm], W=[kd])
                ACTF(kb, vtm[:], h4(pkv[:, 512:1024]), AF.Copy, R=[pkvb], W=[vtm])
                yield
                pM, pMb = psr.next()
                for hd in range(4):
                    MM(kb, pM[:, hd * 128:(hd + 1) * 128], kT(hd), kT(hd), R=[qkv], W=[pMb])
                yield
                pQ, pQb = psr.next()
                for hd in range(4):
                    MM(kb, pQ[:, hd * 128:(hd + 1) * 128], kT(hd), qT(hd), R=[qkv], W=[pQb])
                TT(kb, "dve", qkT[:], h4(pQ[:, :]), Dt[:], ALU.mult, R=[pQb, Dt], W=[qkT])
                TT(kb, "pool", A2[:], Dt[:], SM[:], ALU.mult, R=[Dt, SM], W=[A2])
                TT(kb, "pool", A2[:], A2[:], bc4(nbeta[:, n, :]), ALU.mult, R=[A2, nbeta], W=[A2])
                TT(kb, "dve", P0T[:], h4(pM[:, :]), A2[:], ALU.mult, R=[pMb, A2], W=[P0T])
                yield
                pT, pTb = psr.next()
                for hd in range(4):
                    TR(kb, pT[:, hd * 128:(hd + 1) * 128], P0T[:, hd, :], idf[:], R=[P0T, c["b_id"]], W=[pTb])
                ACTF(kb, P0[:], h4(pT[:, :]), AF.Copy, R=[pTb], W=[P0])
                XT = XTb[0]
                TT(kb, "pool", XT[:], P0T[:], I4[:], ALU.add, R=[P0T, I4], W=[XT])
                Pa, PaT = P0, P0T
                for k in range(1, 7):
                    Pn = Pb[k % 2]
                    yield
                    pP, pPb = psr.next()
                    for hd in range(4):
                        MM(kb, pP[:, hd * 128:(hd + 1) * 128], PaT[:, hd, :], Pa[:, hd, :], R=[PaT, Pa], W=[pPb])
                    ACTF(kb, Pn[:], h4(pP[:, :]), AF.Copy, R=[pPb], W=[Pn])
                    PnT = None
                    if k < 6:
                        PnT = PTb[k % 2]
                        yield
                        pP2, pP2b = psr.next()
                        for hd in range(4):
                            MM(kb, pP2[:, hd * 128:(hd + 1) * 128], Pa[:, hd, :], PaT[:, hd, :], R=[PaT, Pa], W=[pP2b])
                        kb.op("dve", lambda e: e.tensor_copy(out=PnT[:], in_=h4(pP2[:, :])), R=[pP2b], W=[PnT])
                    yield
                    pX, pXb = psr.next()
                    for hd in range(4):
                        MM(kb, pX[:, hd * 128:(hd + 1) * 128], Pn[:, hd, :], XT[:, hd, :], R=[Pn, XT], W=[pXb])
                    XTn = XTbf if k == 6 else XTb[k % 2]
                    TT(kb, "dve", XTn[:], XT[:], h4(pX[:, :]), ALU.add, R=[XT, pXb], W=[XTn])
                    Pa, PaT, XT = Pn, PnT, XTn
                yield

            def stage2(n):
                par = n % 2
                cs = slice(n * 128, (n + 1) * 128)
                qkv = qkvs[n % 2]
                zt = zts[n % 2]
                qT = lambda hd: qkv[:, 0, hd, :]
                kT = lambda hd: qkv[:, 1, hd, :]
                sm, kd, vtm, qkT, XTbf = sm2[par], kd2[par], vtm2[par], qkT2[par], XTbf2[par]
                yield
                pKS, pKSb = psr.next()
                for hd in range(4):
                    MM(kb, pKS[:, hd * 128:(hd + 1) * 128], kT(hd), Sbf[:, hd, :], R=[qkv, Sbf], W=[pKSb])
                TT(kb, "dve", tmp[:], h4(pKS[:, :]), bc4(sm[:, 2, :]), ALU.mult, R=[pKSb, sm], W=[tmp])
                TT(kb, "pool", Tm[:], tmp[:], vtm[:], ALU.add, R=[tmp, vtm], W=[Tm])
                yield
                pU, pUb = psr.next()
                for hd in range(4):
                    MM(kb, pU[:, hd * 128:(hd + 1) * 128], XTbf[:, hd, :], Tm[:, hd, :], R=[XTbf, Tm], W=[pUb])
                TT(kb, "dve", u[:], h4(pU[:, :]), bc4(beta[:, n, :]), ALU.mult, R=[pUb, beta], W=[u])
                yield
                pO1, pO1b = psr.next()
                for hd in range(4):
                    MM(kb, pO1[:, hd * 128:(hd + 1) * 128], qT(hd), Sbf[:, hd, :], R=[qkv, Sbf], W=[pO1b])
                yield
                pO2, pO2b = psr.next()
                for hd in range(4):
                    MM(kb, pO2[:, hd * 128:(hd + 1) * 128], qkT[:, hd, :], u[:, hd, :], R=[qkT, u], W=[pO2b])
                TT(kb, "dve", o1[:], h4(pO1[:, :]), bc4(sm[:, 1, :]), ALU.mult, R=[pO1b, sm], W=[o1])
                TT(kb, "dve", o[:], o1[:], h4(pO2[:, :]), ALU.add, R=[o1, pO2b], W=[o])
                yield
                pdS, pdSb = psr.next()
                for hd in range(4):
                    MM(kb, pdS[:, hd * 128:(hd + 1) * 128], kd[:, hd, :], u[:, hd, :], R=[kd, u], W=[pdSb])
                for hd in range(4):
                    STT(kb, "dve", St[:, hd, :], St[:, hd, :], sm[:, 4, hd:hd + 1], pdS[:, hd * 128:(hd + 1) * 128],
                        ALU.mult, ALU.add, R=[St, sm, pdSb], W=[St])
                ACTF(kb, Sbf[:], St[:], AF.Copy, R=[St], W=[Sbf])
                for hd in range(4):
                    ACTF(kb, junk[:, hd, :], o[:, hd, :], AF.Square, R=[o], W=[junk, ss], accum_out=ss[:, hd:hd + 1])
                ACTF(kb, ss[:, 4:8], ss[:, 0:4], AF.Sqrt, R=[ss], W=[ss], bias=_EPS_AP["ap"], scale=1.0 / 128)
                kb.op("dve", lambda e: e.reciprocal(out=ss[:, 8:12], in_=ss[:, 4:8]), R=[ss], W=[ss])
                TT(kb, "dve", on[:], o[:], bc4(ss[:, 8:12]), ALU.mult, R=[o, ss], W=[on])
                TT(kb, "pool", zg[:], h4(zt[:]), ng[:].unsqueeze(1).to_broadcast([128, 4, 128]), ALU.mult, R=[zt, ng], W=[zg])
                TT(kb, "pool", of[:], on[:], zg[:], ALU.mult, R=[on, zg], W=[of])
                yield
                pF, pFb = psb.next()
                for hd in range(4):
                    TR(kb, pF[:, hd * 128:(hd + 1) * 128], of[:, hd, :], idb[:], R=[of, c["b_id"]], W=[pFb])
                ACTF(kb, oaT[:, :, cs], h4(pF[:, 0:512]), AF.Copy, R=[pFb], W=[oaT])
                yield

            def run_il(g2, g1, ratio=3):
                d1 = d2 = False
                while not (d1 and d2):
                    if not d2:
                        try:
                            next(g2)
                        except StopIteration:
                            d2 = True
                    for _ in range(ratio):
                        if not d1:
                            try:
                                next(g1)
                            except StopIteration:
                                d1 = True

            for _ in stage1(0):
                pass
            for n in range(NT):
                if GDN_IL:
                    run_il(stage2(n), stage1(n + 1) if n + 1 < NT else iter(()))
                else:
                    for _ in stage2(n):
                        pass
                    if n + 1 < NT:
                        for _ in stage1(n + 1):
                            pass
            for hd in range(4):
                kb.dma("sp", oT_d[hd, :, :], oaT[:, hd, :], R=[oaT], W=oT_b[hd])
        with Scope(kb, "ab_out") as sc:
            wo = sbt(sc, "ab_wo", [128, 8, D], BF16)
            for n0 in range(0, 8, 2):
                kb.dma("pool", wo[:, n0:n0 + 2, :], P["ab_w_out"][i][n0 * 128:(n0 + 2) * 128, :].rearrange("(c p) n -> p c n", p=128), W=[wo])
            proj_tm_residual(kb, sc, oT_d, oT_b, 8, wo.t, wo.b, h_src, h_dst, "abo")


PATS = ((128, 1), (512, 4), (2048, 16))


def perm_cols(ap2d, d, u0, n):
    L = T // d
    v = ap2d.rearrange("p (j r) -> p r j", r=d)
    r0, j0 = u0 // L, u0 % L
    if j0 + n <= L:
        return v[:, r0, j0:j0 + n]
    nr = n // L
    return v[:, r0:r0 + nr, :]


def cd_layer(kb, c, P, i, layer, h_src, h_dst, S):
    w_in = P["cd_w_in"][i]
    qeT_d, keT_d, v_d, gt_d, oT_d = S["qT_d"], S["kT_d"], S["v_d"], S["z_d"], S["oT_d"]
    qe_b, ke_b = bufs(4), bufs(4)
    v_b, gt_b = bufs(NT), bufs(NT)
    oT_b = [bufs(8) for _ in range(8)]
    QKT_d, Vp_d, oacc_d = S["QKT_d"], S["Vp_d"], S["oacc_d"]
    QKT_b = [[bufs(8) for _ in range(8)] for _ in range(3)]
    Vp_b = [bufs(NT) for _ in range(3)]
    oacc_b = [bufs(NT) for _ in range(3)]
    idf, idb = c["idf"], c["idb"]
    with Scope(kb) as lsc:
        hsc = sbt(lsc, "cd_hsc", [128, 4, 2, 64], F32)
        with Scope(kb) as osc:
            hnT = osc.sb("cd_hnT", [128, 8, T], BF16)
            hnT_b = bufs(NT)
            with Scope(kb, "cd_norm") as sc:
                psr_t = PsumRing(sc, "cd_pt", 2, (128, 1024), BF16)
                norm_transpose(kb, sc, c, h_src, P["norm_mix_g"][layer], hnT, hnT_b, psr_t, "cdn")
            with Scope(kb, "cd_hproj") as sc:
                psr = PsumRing(sc, "cdh_pp", 8)
                wts = [sbt(sc, "cdh_w%d" % j, [128, 8, 128], BF16) for j in range(2)]
                wi = [0]

                def next_w(c0):
                    w = wts[wi[0] % 2]
                    wi[0] += 1
                    load_w_cols(kb, w[:], w, w_in, c0, 128)
                    return w

                lbt = sbt(sc, "cdh_lb", [128, 4, 2], F32)
                kb.dma("sp", lbt[:], P["hg_lbT"].rearrange("(c p) j -> p c j", p=128), W=[lbt])
                lb2 = sbt(sc, "cdh_lb2", [128, 4, 2], F32)
                if i == 0:
                    kb.op("pool", lambda e: e.memset(lb2[:, :, 0], 1.0), W=[lb2])
                    kb.op("pool", lambda e: e.memset(lb2[:, :, 1], 0.0), W=[lb2])
                else:
                    TT(kb, "dve", lb2[:, :, 1], lbt[:, :, 1], lbt[:, :, 0], ALU.subtract, R=[lbt], W=[lb2])
                    ACTF(kb, lb2[:, :, 1], lb2[:, :, 1], AF.Sigmoid, R=[lb2], W=[lb2])
                    TS(kb, "dve", lb2[:, :, 0], lb2[:, :, 1], -1.0, ALU.mult, R=[lb2], W=[lb2], s2=1.0, op1=ALU.add)
                segm = sbt(sc, "cdh_segm", [128, T], F32)
                kb.op("pool", lambda e: e.memset(segm[:], 1.0), W=[segm])
                kb.op("pool", lambda e: e.memset(segm[:].rearrange("p (n c) -> p n c", c=64)[:, :, 0:1], 0.0), R=[segm], W=[segm])
                A = sbt(sc, "cdh_A", [128, T], F32)
                B = sbt(sc, "cdh_B", [128, T], F32)
                C = sbt(sc, "cdh_C", [128, T], F32)
                orow = [sbt(sc, "cdh_or%d" % j, [128, T], BF16) for j in range(2)]
                tq = [sbt(sc, "cdh_tq%d" % j, [128, 512], F32) for j in range(2)]
                v3 = lambda t: t[:].rearrange("p (n c) -> p n c", c=64)
                for hd in range(4):
                    w = next_w(512 + hd * 128)

                    def epi_f(tt, pt, pb):
                        ACTF(kb, A[:, tt * 512:(tt + 1) * 512], pt[:], AF.Sigmoid, R=[pb], W=[A])
                        ACTF(kb, B[:, tt * 512:(tt + 1) * 512], pt[:], AF.Sigmoid, R=[pb], W=[B], scale=-1.0)
                    proj_fm_chunk(kb, hnT, hnT_b, w, psr, epi_f)
                    TS(kb, "dve", A[:], A[:], lb2[:, hd, 0:1], ALU.mult, R=[A, lb2], W=[A], s2=lb2[:, hd, 1:2], op1=ALU.add)
                    ACTF(kb, A[:], A[:], AF.Ln, R=[A], W=[A])
                    TS(kb, "dve", B[:], B[:], lb2[:, hd, 0:1], ALU.mult, R=[B, lb2], W=[B])
                    kb.op("dve", lambda e: e.tensor_tensor_scan(out=C[:], data0=segm[:], data1=A[:], initial=0.0,
                                                                op0=ALU.mult, op1=ALU.add), R=[segm, A], W=[C])
                    TT(kb, "dve", v3(A), v3(C), v3(C)[:, :, 31:32].to_broadcast([128, 64, 64]), ALU.subtract, R=[C], W=[A])
                    ACTF(kb, hsc[:, hd, 0, :], v3(C)[:, :, 31], AF.Exp, R=[C], W=[hsc])
                    TT(kb, "dve", hsc[:, hd, 1, :], v3(C)[:, :, 63], v3(C)[:, :, 31], ALU.subtract, R=[C], W=[hsc])
                    ACTF(kb, hsc[:, hd, 1, :], hsc[:, hd, 1, :], AF.Exp, R=[hsc], W=[hsc])
                    TS(kb, "dve", C[:], A[:], 40.0, ALU.min, R=[A], W=[C])
                    ACTF(kb, C[:], C[:], AF.Exp, R=[C], W=[C])
                    w = next_w(hd * 128)
                    oq = orow[0]

                    def epi_q(tt, pt, pb):
                        t = tq[tt % 2]
                        ACTF(kb, t[:], pt[:], AF.Silu, R=[pb], W=[t])
                        TT(kb, "dve", oq[:, tt * 512:(tt + 1) * 512], t[:], C[:, tt * 512:(tt + 1) * 512], ALU.mult, R=[t, C], W=[oq])
                    proj_fm_chunk(kb, hnT, hnT_b, w, psr, epi_q)
                    kb.dma("sp", qeT_d[hd, :, :], oq[:], R=[oq], W=[qe_b[hd]])
                    TS(kb, "dve", C[:], A[:], -1.0, ALU.mult, R=[A], W=[C], s2=40.0, op1=ALU.min)
                    ACTF(kb, C[:], C[:], AF.Exp, R=[C], W=[C])
                    ok = orow[1]
                    TT(kb, "dve", ok[:], B[:], C[:], ALU.mult, R=[B, C], W=[ok])
                    kb.dma("sp", keT_d[hd, :, :], ok[:], R=[ok], W=[ke_b[hd]])
                wz = sbt(sc, "cdh_wz", [128, 8, 1024], BF16)
                load_w_cols(kb, wz[:, :, 0:512], wz, w_in, 1024, 512)
                load_w_cols(kb, wz[:, :, 512:1024], wz, w_in, 1536, 512)
                zts = [sbt(sc, "cdh_zt%d" % j, [128, 1024], BF16) for j in range(2)]
                for it in range(NT):
                    zt = zts[it % 2]
                    for half in range(2):
                        pt, pb = psr.next()
                        for dc in range(8):
                            MM(kb, pt[:], hnT[:, dc, it * 128:(it + 1) * 128], wz[:, dc, half * 512:(half + 1) * 512],
                               R=[hnT_b[it], wz], W=[pb], start=(dc == 0), stop=(dc == 7))
                        ACTF(kb, zt[:, half * 512:(half + 1) * 512], pt[:], AF.Copy if half == 0 else AF.Sigmoid, R=[pb], W=[zt])
                    kb.dma("sp", v_d[it * 128:(it + 1) * 128, :], zt[:, 0:512], R=[zt], W=[v_b[it]])
                    kb.dma("sp", gt_d[it * 128:(it + 1) * 128, :], zt[:, 512:1024], R=[zt], W=[gt_b[it]])
            with Scope(kb, "cd_aproj") as sc:
                psr = PsumRing(sc, "cda_pp", 8)
                psw = sbt(sc, "cda_psw", [128, 128], BF16)
                psw2 = sbt(sc, "cda_psw2", [128, 128], BF16)
                kb.op("pool", lambda e: e.memset(psw[:], 1.0), W=[psw])
                kb.op("pool", lambda e: e.affine_select(out=psw[:], in_=psw[:], pattern=[[-1, 128]], compare_op=ALU.is_equal,
                                                        fill=0.0, base=-64, channel_multiplier=1), R=[psw], W=[psw])
                kb.op("pool", lambda e: e.memset(psw2[:], 1.0), W=[psw2])
                kb.op("pool", lambda e: e.affine_select(out=psw2[:], in_=psw2[:], pattern=[[-1, 128]], compare_op=ALU.is_equal,
                                                        fill=0.0, base=64, channel_multiplier=1), R=[psw2], W=[psw2])
                TT(kb, "pool", psw[:], psw[:], psw2[:], ALU.add, R=[psw, psw2], W=[psw])
                wqk = sbt(sc, "cda_wqk", [128, 8, 1024], BF16)
                wv = sbt(sc, "cda_wv", [128, 8, 512], BF16)
                cs_t = [sbt(sc, "cda_cs%d" % j, [128, 2, 512], F32) for j in range(3)]
                xb = [sbt(sc, "cda_xb%d" % j, [128, 512], BF16) for j in range(3)]
                t1 = [sbt(sc, "cda_t1%d" % j, [128, 512], F32) for j in range(2)]
                t2 = [sbt(sc, "cda_t2%d" % j, [128, 512], F32) for j in range(2)]
                ob = [sbt(sc, "cda_ob%d" % j, [128, 512], BF16) for j in range(3)]
                vb = [sbt(sc, "cda_vb%d" % j, [128, 512], BF16) for j in range(2)]
                k = 0
                pend = None
                for p, (win, d) in enumerate(PATS):
                    load_w_cols(kb, wqk[:, :, 0:512], wqk, w_in, 2048 + p * 512, 512)
                    load_w_cols(kb, wqk[:, :, 512:1024], wqk, w_in, 3584 + p * 512, 512)
                    load_w_cols(kb, wv[:], wv, w_in, 5120 + p * 512, 512)
                    for tt in range(8):
                        cst = cs_t[tt % 3]
                        kb.dma("sp", cst[:, 0, :], P["rope_cos"][p, :, tt * 512:(tt + 1) * 512], W=[cst])
                        kb.dma("sp", cst[:, 1, :], P["rope_sin"][p, :, tt * 512:(tt + 1) * 512], W=[cst])
                        hb = hnT_b
                        def rope_tail(x_, cst, ch, tt, kk):
                            a1, a2, o_ = t1[kk % 2], t2[kk % 2], ob[kk % 3]
                            p2, p2b = psr.next()
                            MM(kb, p2[:], psw[:], x_[:], R=[psw, x_], W=[p2b])
                            TT(kb, "pool", a1[:], x_[:], cst[:, 0, :], ALU.mult, R=[x_, cst], W=[a1])
                            TT(kb, "dve", a2[:], p2[:], cst[:, 1, :], ALU.mult, R=[p2b, cst], W=[a2])
                            TT(kb, "dve", o_[:], a1[:], a2[:], ALU.add, R=[a1, a2], W=[o_])
                            kb.dma("sp", QKT_d[p, ch, :, tt * 512:(tt + 1) * 512], o_[:], R=[o_], W=[QKT_b[p][ch][tt]])

                        for ch in range(8):
                            pt, pb = psr.next()
                            for dc in range(8):
                                MM(kb, pt[:], wqk[:, dc, ch * 128:(ch + 1) * 128], perm_cols(hnT[:, dc, :], d, tt * 512, 512),
                                   R=[wqk] + hb, W=[pb], start=(dc == 0), stop=(dc == 7))
                            x_ = xb[k % 3]
                            ACTF(kb, x_[:], pt[:], AF.Copy, R=[pb], W=[x_])
                            if pend is not None:
                                rope_tail(*pend)
                            pend = (x_, cst, ch, tt, k)
                            k += 1
                            if not ROPE_PIPE:
                                rope_tail(*pend)
                                pend = None
                    if pend is not None:
                        rope_tail(*pend)
                        pend = None
                    for n in range(NT):
                        pt, pb = psr.next()
                        for dc in range(8):
                            MM(kb, pt[:], perm_cols(hnT[:, dc, :], d, n * 128, 128), wv[:, dc, :], R=[wv] + hnT_b, W=[pb],
                               start=(dc == 0), stop=(dc == 7))
                        v_ = vb[n % 2]
                        ACTF(kb, v_[:], pt[:], AF.Copy, R=[pb], W=[v_])
                        kb.dma("sp", Vp_d[p, n * 128:(n + 1) * 128, :], v_[:], R=[v_], W=[Vp_b[p][n]])
        with Scope(kb, "hgrn") as sc:
            psr = PsumRing(sc, "hg_p", 6)
            psb = PsumRing(sc, "hg_pb", 2, (128, 1024), BF16)
            BD = sbt(sc, "hg_BD", [128, 4, 128], F32)
            kb.op("pool", lambda e: e.memset(BD[:], 1.0), W=[BD])
            affsel(kb, BD[:], [[0, 4], [1, 128]], -1, ALU.is_ge, 0.0, [BD])
            kb.op("pool", lambda e: e.memset(BD[0:64, :, 64:128], 0.0), R=[BD], W=[BD])
            ng = sbt(sc, "hg_ng", [128, 128], F32)
            kb.dma("sp", ng[:], bc_row(P["hg_norm_g"][i], 128), W=[ng])
            fac = sbt(sc, "hg_fac", [128, 4, 64], F32)
            kb.op("pool", lambda e: e.memset(fac[:], 1.0), W=[fac])
            TT(kb, "dve", fac[:, :, 0:63], hsc[:, :, 1, 0:63], hsc[:, :, 0, 1:64], ALU.mult, R=[hsc, fac], W=[fac])

            def f4(name, dt=F32):
                return sbt(sc, "hg_" + name, [128, 4, 128], dt)

            qes = [f4("qe%d" % j, BF16) for j in range(2)]
            kes = [f4("ke%d" % j, BF16) for j in range(2)]
            vts = [f4("v%d" % j, BF16) for j in range(2)]
            gts = [f4("g%d" % j, BF16) for j in range(2)]
            qeAB = sbt(sc, "hg_qeAB", [128, 4, 2, 128], BF16)
            keAB = sbt(sc, "hg_keAB", [128, 4, 2, 128], BF16)
            kb.op("pool", lambda e: e.memset(qeAB[:], 0.0), W=[qeAB])
            kb.op("pool", lambda e: e.memset(keAB[:], 0.0), W=[keAB])
            attT = f4("attT", BF16)
            St, tmp, o, on, zg, junk = f4("S"), f4("tmp"), f4("o"), f4("on"), f4("zg"), f4("junk")
            Sbf, of = f4("Sbf", BF16), f4("of", BF16)
            kb.op("pool", lambda e: e.memset(St[:], 0.0), W=[St])
            kb.op("pool", lambda e: e.memset(Sbf[:], 0.0), W=[Sbf])
            ss = sbt(sc, "hg_ss", [128, 12], F32)
            ocT = sbt(sc, "hg_ocT", [128, 4, T], BF16)
            for n in range(NT):
                cs = slice(n * 128, (n + 1) * 128)
                qe, ke, vt, gt = qes[n % 2], kes[n % 2], vts[n % 2], gts[n % 2]
                kb.dma("sp", qe[:], qeT_d[:, :, cs].rearrange("h p n -> p h n"), R=qe_b, W=[qe])
                kb.dma("sp", ke[:], keT_d[:, :, cs].rearrange("h p n -> p h n"), R=ke_b, W=[ke])
                kb.dma("sp", vt[:], h4(v_d[cs, :]), R=[v_b[n]], W=[vt])
                kb.dma("sp", gt[:], h4(gt_d[cs, :]), R=[gt_b[n]], W=[gt])
                kb.op("pool", lambda e: e.tensor_copy(out=qeAB[:, :, 0, 0:64], in_=qe[:, :, 0:64]), R=[qe], W=[qeAB])
                kb.op("pool", lambda e: e.tensor_copy(out=qeAB[:, :, 1, 64:128], in_=qe[:, :, 64:128]), R=[qe], W=[qeAB])
                pk, pkb = psb.next()
                for hd in range(4):
                    TR(kb, pk[:, hd * 128:(hd + 1) * 128], ke[:, hd, :], idb[:], R=[ke, c["b_id"]], W=[pkb])
                ACTF(kb, keAB[0:64, :, 0, :], h4(pk[0:64, 0:512]), AF.Copy, R=[pkb], W=[keAB])
                ACTF(kb, keAB[64:128, :, 1, :], h4(pk[64:128, 0:512]), AF.Copy, R=[pkb], W=[keAB])
                pA, pAb = psr.next()
                for hd in range(4):
                    MM(kb, pA[:, hd * 128:(hd + 1) * 128], ke[:, hd, :], qe[:, hd, :], R=[ke, qe], W=[pAb])
                TT(kb, "dve", attT[:], h4(pA[:, :]), BD[:], ALU.mult, R=[pAb, BD], W=[attT])
                pOs = []
                for half in range(2):
                    ck = 2 * n + half
                    pO, pOb = psr.next()
                    pOs.append((pO, pOb))
                    for hd in range(4):
                        if half == 0:
                            MM(kb, pO[:, hd * 128:(hd + 1) * 128], attT[:, hd, :], vt[:, hd, :], R=[attT, vt], W=[pOb],
                               start=True, stop=False)
                        MM(kb, pO[:, hd * 128:(hd + 1) * 128], qeAB[:, hd, half, :], Sbf[:, hd, :], R=[qeAB, Sbf], W=[pOb],
                           start=(half == 1), stop=True)
                    pK, pKb = psr.next()
                    for hd in range(4):
                        MM(kb, pK[:, hd * 128:(hd + 1) * 128], keAB[:, hd, half, :], vt[:, hd, :], R=[keAB, vt], W=[pKb])
                    fb = fac[:, :, ck:ck + 1].to_broadcast([128, 4, 128])
                    TT(kb, "dve", St[:], St[:], fb, ALU.mult, R=[St, fac], W=[St])
                    TT(kb, "dve", tmp[:], h4(pK[:, :]), fb, ALU.mult, R=[pKb, fac], W=[tmp])
                    TT(kb, "dve", St[:], St[:], tmp[:], ALU.add, R=[St, tmp], W=[St])
                    ACTF(kb, Sbf[:], St[:], AF.Copy, R=[St], W=[Sbf])
                ACTF(kb, o[:], h4(pOs[0][0][:, :]), AF.Copy, R=[pOs[0][1]], W=[o])
                TT(kb, "dve", o[:], o[:], h4(pOs[1][0][:, :]), ALU.add, R=[o, pOs[1][1]], W=[o])
                for hd in range(4):
                    ACTF(kb, junk[:, hd, :], o[:, hd, :], AF.Square, R=[o], W=[junk, ss], accum_out=ss[:, hd:hd + 1])
                ACTF(kb, ss[:, 4:8], ss[:, 0:4], AF.Sqrt, R=[ss], W=[ss], bias=_EPS_AP["ap"], scale=1.0 / 128)
                kb.op("dve", lambda e: e.reciprocal(out=ss[:, 8:12], in_=ss[:, 4:8]), R=[ss], W=[ss])
                TT(kb, "dve", on[:], o[:], bc4(ss[:, 8:12]), ALU.mult, R=[o, ss], W=[on])
                TT(kb, "pool", zg[:], gt[:], ng[:].unsqueeze(1).to_broadcast([128, 4, 128]), ALU.mult, R=[gt, ng], W=[zg])
                TT(kb, "pool", of[:], on[:], zg[:], ALU.mult, R=[on, zg], W=[of])
                pF, pFb = psb.next()
                for hd in range(4):
                    TR(kb, pF[:, hd * 128:(hd + 1) * 128], of[:, hd, :], idb[:], R=[of, c["b_id"]], W=[pFb])
                ACTF(kb, ocT[:, :, cs], h4(pF[:, 0:512]), AF.Copy, R=[pFb], W=[ocT])
            for hd in range(4):
                kb.dma("sp", oT_d[hd, :, :], ocT[:, hd, :], R=[ocT], W=oT_b[hd])
        with Scope(kb, "attn") as sc:
            psr = PsumRing(sc, "at_p", 4)
            pso = PsumRing(sc, "at_po", 4)
            mkf = sbt(sc, "at_mkf", [128, 256], F32)
            mkb = sbt(sc, "at_mkb", [128, 256], BF16)
            kb.op("pool", lambda e: e.memset(mkf[:], 0.0), W=[mkf])
            affsel(kb, mkf[:, 0:128], [[-1, 128]], 1, ALU.is_ge, -30000.0, [mkf])
            affsel(kb, mkf[:, 128:256], [[1, 128]], -1, ALU.is_ge, -30000.0, [mkf])
            kb.op("pool", lambda e: e.tensor_copy(out=mkb[:], in_=mkf[:]), R=[mkf], W=[mkb])
            QT = sbt(sc, "at_QT", [128, 4, T], BF16)
            KT = sbt(sc, "at_KT", [128, 4, T], BF16)
            Va = sbt(sc, "at_Va", [128, NT, 4, 129], BF16)
            kb.op("pool", lambda e: e.memset(Va[:, :, :, 128:129], 1.0), W=[Va])
            pTs = [sbt(sc, "at_pT%d" % j, [128, 256], BF16) for j in range(3)]
            ost = [sbt(sc, "at_os%d" % j, [128, 4, 129], F32) for j in range(2)]
            sc_ = float(128 ** -0.5)
            k = 0
            apend = None

            def attn_tail(pT, n, hd, first, os_):
                po, pob = pso.next()
                if not first:
                    MM(kb, po[:, 0:129], pT[:, 0:128], Va[:, n - 1, hd, :], R=[pT, Va], W=[pob], start=True, stop=False)
                MM(kb, po[:, 0:129], pT[:, 128:256], Va[:, n, hd, :], R=[pT, Va], W=[pob], start=first, stop=True)
                kb.op("dve", lambda e: e.tensor_copy(out=os_[:, hd, :], in_=po[:, 0:129]), R=[pob], W=[os_])

            for p, (win, d) in enumerate(PATS):
                nb = (T // d) // 128
                for hd in range(4):
                    kb.dma("sp", QT[:, hd, :], QKT_d[p, hd, :, :], R=QKT_b[p][hd], W=[QT])
                    kb.dma("sp", KT[:, hd, :], QKT_d[p, 4 + hd, :, :], R=QKT_b[p][4 + hd], W=[KT])
                    kb.dma("sp", Va[:, :, hd, 0:128], Vp_d[p, :, hd * 128:(hd + 1) * 128].rearrange("(n p) c -> p n c", p=128),
                           R=Vp_b[p], W=[Va])
                for n in range(NT):
                    first = (n % nb == 0)
                    os_ = ost[n % 2]
                    for hd in range(4):
                        ps, psb_ = psr.next()
                        q_ = QT[:, hd, n * 128:(n + 1) * 128]
                        if not first:
                            MM(kb, ps[:, 0:128], KT[:, hd, (n - 1) * 128:n * 128], q_, R=[KT, QT], W=[psb_], start=True, stop=False)
                            MM(kb, ps[:, 0:128], idb[:], mkb[:, 0:128], R=[mkb, c["b_id"]], W=[psb_], start=False, stop=True)
                        MM(kb, ps[:, 128:256], KT[:, hd, n * 128:(n + 1) * 128], q_, R=[KT, QT], W=[psb_], start=True, stop=False)
                        MM(kb, ps[:, 128:256], idb[:], mkb[:, 128:256], R=[mkb, c["b_id"]], W=[psb_], start=False, stop=True)
                        pT = pTs[k % 3]
                        k += 1
                        lo = 128 if first else 0
                        ACTF(kb, pT[:, lo:256], ps[:, lo:256], AF.Exp, R=[psb_], W=[pT], scale=sc_)
                        if apend is not None:
                            attn_tail(*apend)
                        apend = (pT, n, hd, first, os_)
                        if not ATT_PIPE:
                            attn_tail(*apend)
                            apend = None
                    if apend is not None:
                        attn_tail(*apend)
                        apend = None
                    r, jb = n // nb, n % nb
                    dst = oacc_d[p].rearrange("(j r) c -> r j c", r=d)[r, jb * 128:(jb + 1) * 128, :]
                    tok0 = r + d * jb * 128
                    tiles = sorted(set((tok0 + d * ii) // 128 for ii in (0, 127)) | set(range((tok0) // 128, (tok0 + d * 127) // 128 + 1)))
                    kb.dma("sp", dst, os_[:].rearrange("p h c -> p (h c)"), R=[os_], W=[oacc_b[p][t_] for t_ in tiles])
        with Scope(kb, "merge") as sc:
            psb = PsumRing(sc, "mg_pb", 2, (128, 1024), BF16)
            acs = [[sbt(sc, "mg_a%d_%d" % (p, j), [128, 4, 129], F32) for p in range(3)] for j in range(2)]
            rl = sbt(sc, "mg_rl", [128, 4], F32)
            od = sbt(sc, "mg_od", [128, 4, 128], BF16)
            odT = sbt(sc, "mg_odT", [128, 4, T], BF16)
            for n in range(NT):
                a = acs[n % 2]
                for p in range(3):
                    kb.dma("sp", a[p][:].rearrange("p h c -> p (h c)"), oacc_d[p][n * 128:(n + 1) * 128, :], R=[oacc_b[p][n]], W=[a[p]])
                TT(kb, "pool", a[0][:], a[0][:], a[1][:], ALU.add, R=[a[0], a[1]], W=[a[0]])
                TT(kb, "dve", a[0][:], a[0][:], a[2][:], ALU.add, R=[a[0], a[2]], W=[a[0]])
                kb.op("dve", lambda e: e.reciprocal(out=rl[:], in_=a[0][:, :, 128]), R=[a[0]], W=[rl])
                TT(kb, "dve", od[:], a[0][:, :, 0:128], bc4(rl[:]), ALU.mult, R=[a[0], rl], W=[od])
                pF, pFb = psb.next()
                for hd in range(4):
                    TR(kb, pF[:, hd * 128:(hd + 1) * 128], od[:, hd, :], idb[:], R=[od, c["b_id"]], W=[pFb])
                ACTF(kb, odT[:, :, n * 128:(n + 1) * 128], h4(pF[:, 0:512]), AF.Copy, R=[pFb], W=[odT])
            for hd in range(4):
                kb.dma("sp", oT_d[4 + hd, :, :], odT[:, hd, :], R=[odT], W=oT_b[4 + hd])
        with Scope(kb, "cd_out") as sc:
            wo = sbt(sc, "cd_wo", [128, 8, D], BF16)
            for n0 in range(0, 8, 2):
                kb.dma("pool", wo[:, n0:n0 + 2, :], P["cd_w_out"][i][n0 * 128:(n0 + 2) * 128, :].rearrange("(c p) n -> p c n", p=128), W=[wo])
            proj_tm_residual(kb, sc, oT_d, oT_b, 8, wo.t, wo.b, h_src, h_dst, "cdo")


def build(n_layers=DEPTH, do_mixer=True, do_ffn=True, dbg_h=False):
    _HB.clear()
    nc = bass.Bass("TRN2", target_bir_lowering=False)

    def din(name, shape, dt=F32):
        return nc.dram_tensor(name, list(shape), dt, kind="ExternalInput").ap()

    def dscr(name, shape, dt):
        return nc.dram_tensor(name, list(shape), dt, kind="Internal").ap()

    P = {}
    x = din("x", [T, D])
    for name, shape in (("norm_mix_g", [DEPTH, D]), ("norm_ffn_g", [DEPTH, D]), ("norm_final_g", [D]),
                        ("ffn_w_gate", [DEPTH, D, FH]), ("ffn_w_up", [DEPTH, D, FH]), ("ffn_w_down", [DEPTH, FH, D]),
                        ("ab_w_in", [2, D, AB_IN]), ("dn_conv_wT", [2, 1536, 4]), ("dn_a_log", [2, 4]), ("dn_dt_bias", [2, 4]),
                        ("dn_norm_g", [2, 128]), ("sc_conv_wT", [2, 512, 3]), ("ab_w_out", [2, D, D]),
                        ("cd_w_in", [2, D, CD_IN]), ("hg_lbT", [512, 2]), ("hg_norm_g", [2, 128]), ("cd_w_out", [2, D, D]),
                        ("rope_cos", [3, 128, T]), ("rope_sin", [3, 128, T])):
        P[name] = din(name, shape)
    out = nc.dram_tensor("out", [T, D], F32, kind="ExternalOutput").ap()
    h = dscr("h_res", [T, D], F32)
    dbg_oT = nc.dram_tensor("dbg_oT", [8, 128, T], BF16, kind="ExternalOutput").ap() if dbg_h else None
    S = {"actT_d": dscr("actT_d", [NFC, 128, T], BF16),
         "qT_d": dscr("qT_d", [4, 128, T], BF16), "kT_d": dscr("kT_d", [4, 128, T], BF16), "vT_d": dscr("vT_d", [4, 128, T], BF16),
         "z_d": dscr("z_d", [T, 512], BF16), "oT_d": dscr("oT_d", [8, 128, T], BF16),
         "v_d": dscr("v_d", [T, 512], BF16), "QKT_d": dscr("QKT_d", [3, 8, 128, T], BF16),
         "Vp_d": dscr("Vp_d", [3, T, 512], BF16), "oacc_d": dscr("oacc_d", [3, T, 516], F32)}

    with ExitStack() as es:
        kb = KB(nc, es)
        with Scope(kb) as gsc:
            c = make_consts(kb, gsc)
            make_eps(kb, gsc)
            cur = x
            for layer in range(n_layers):
                if do_mixer:
                    if layer % 2 == 0:
                        ab_layer(kb, c, P, layer // 2, layer, cur, h, S)
                    else:
                        cd_layer(kb, c, P, layer // 2, layer, cur, h, S)
                    cur = h
                if do_ffn:
                    ffn_layer(kb, c, cur, h, P["norm_ffn_g"][layer], P["ffn_w_gate"][layer], P["ffn_w_up"][layer],
                              P["ffn_w_down"][layer], S["actT_d"], "f%d" % layer)
                    cur = h
            if dbg_h:
                with Scope(kb) as sc:
                    ht = sbt(sc, "dbg_h", [128, D], F32)
                    for i in range(NT):
                        kb.dma("sp", ht[:], cur[i * 128:(i + 1) * 128, :], R=[HB(cur, i)], W=[ht])
                        kb.dma("sp", out[i * 128:(i + 1) * 128, :], ht[:], R=[ht], W=[HB(out, i)])
                    ob = sbt(sc, "dbg_o", [128, T], BF16)
                    for r in range(8):
                        kb.dma("sp", ob[:], S["oT_d"][r, :, :], W=[ob])
                        kb.dma("sp", dbg_oT[r, :, :], ob[:], R=[ob], W=[ob])
            else:
                final_norm(kb, cur, P["norm_final_g"], out)
            kb.finish()
        print("instructions:", kb.n_inst, flush=True)
        build.marks = kb.marks
    return nc


def prep_inputs(inputs):
    f = lambda k: np.ascontiguousarray(np.asarray(inputs[k], dtype=np.float32))
    shared = {k: f(k) for k in ("norm_mix_g", "norm_ffn_g", "norm_final_g", "ffn_w_gate", "ffn_w_up", "ffn_w_down",
                                "ab_w_in", "dn_a_log", "dn_dt_bias", "dn_norm_g", "ab_w_out")}
    shared["dn_conv_wT"] = np.ascontiguousarray(f("dn_conv_w").transpose(0, 2, 1))
    shared["sc_conv_wT"] = np.ascontiguousarray(f("sc_conv_w").transpose(0, 2, 1))
    for k in ("cd_w_in", "hg_norm_g", "cd_w_out"):
        shared[k] = f(k)
    shared["hg_lbT"] = np.ascontiguousarray(f("hg_lower_bounds").T)
    half = 64
    inv_freq = (10000.0 ** (-np.arange(half, dtype=np.float32) * 2.0 / 128)).astype(np.float32)
    ang = np.arange(T, dtype=np.float32)[:, None] * inv_freq[None, :]
    cosF = np.concatenate([np.cos(ang), np.cos(ang)], axis=1).T.astype(np.float32)
    sinS = np.concatenate([-np.sin(ang), np.sin(ang)], axis=1).T.astype(np.float32)
    rc, rs = [], []
    for win, d in PATS:
        L = T // d
        u = np.arange(T)
        tok = (u % L) * d + (u // L)
        rc.append(cosF[:, tok])
        rs.append(sinS[:, tok])
    shared["rope_cos"] = np.ascontiguousarray(np.stack(rc))
    shared["rope_sin"] = np.ascontiguousarray(np.stack(rs))
    return shared


def kernel(**inputs):
    nc = build()
    shared = prep_inputs(inputs)
    x = np.asarray(inputs["x"], dtype=np.float32)
    in_maps = []
    for ci in range(8):
        m = dict(shared)
        m["x"] = np.ascontiguousarray(x[ci])
        in_maps.append(m)
    res = run_bass_kernel_spmd(nc, in_maps, core_ids=list(range(8)))
    return np.stack([np.asarray(r["out"], dtype=np.float32) for r in res.results], axis=0)
```
